# Optimizing a Trainium2 kernel written in Bass

```python
import jax, jax.numpy as jnp
from jax import lax
import numpy as np

D_MODEL = 1024
BATCH = 16
SEQ = 4096
DEPTH = 1

MEM_LEN = 256
CONV_CH = 512
CONV_GROUPS = 8
CONV_WIDTH = 3
MOBA_HEADS = 8
MOBA_HEAD_DIM = 64
MOBA_WIDTH = MOBA_HEADS * MOBA_HEAD_DIM
MOBA_BLOCK = 256
MOBA_TOPK = 3
MOBA_QCHUNK = 16
MEM_HEADS = 4
MEM_HEAD_DIM = 128
MEM_WIDTH = MEM_HEADS * MEM_HEAD_DIM
N_BRANCH = 3
IN_SPLITS = [CONV_CH, CONV_CH, CONV_CH, MOBA_WIDTH, MOBA_WIDTH, MOBA_WIDTH, MEM_WIDTH,
             D_MODEL, D_MODEL, D_MODEL]
IN_COLS = sum(IN_SPLITS)
D_FF = 2816
FFN_CONV_WIDTH = 3
EPS = 1e-6

kernel_name = "hybrid_gated_conv_moba_memxattn_block"


def rmsnorm(x, g):
    xf = x.astype(jnp.float32)
    y = xf * lax.rsqrt(jnp.mean(xf * xf, axis=-1, keepdims=True) + EPS)
    return y.astype(x.dtype) * g


def causal_dwconv3(u, w, b):
    s = u.shape[1]
    up = jnp.pad(u, ((0, 0), (2, 0), (0, 0)))
    return up[:, :s] * w[0] + up[:, 1:s + 1] * w[1] + up[:, 2:] * w[2] + b


def moba_attention(q, k, v):
    bsz, s, h, dh = q.shape
    L = MOBA_BLOCK
    nb = -(-s // L)
    sp = nb * L
    pad = ((0, 0), (0, sp - s), (0, 0), (0, 0))
    q, k, v = jnp.pad(q, pad), jnp.pad(k, pad), jnp.pad(v, pad)
    qh = q.transpose(0, 2, 1, 3)
    kb = k.reshape(bsz, nb, L, h, dh).transpose(0, 3, 1, 2, 4)
    vb = v.reshape(bsz, nb, L, h, dh).transpose(0, 3, 1, 2, 4)
    kbar = jnp.mean(kb, axis=3)
    n_sel = min(MOBA_TOPK, nb - 1)
    scale = dh ** -0.5
    gather = jax.vmap(jax.vmap(lambda blocks, idx: blocks[idx]))

    def one_chunk(ci):
        start = ci * MOBA_QCHUNK
        own = start // L
        qc = lax.dynamic_slice_in_dim(qh, start, MOBA_QCHUNK, axis=2)
        qpos = start + jnp.arange(MOBA_QCHUNK)
        kpos = own * L + jnp.arange(L)
        k_own = lax.dynamic_index_in_dim(kb, own, axis=2, keepdims=False)
        v_own = lax.dynamic_index_in_dim(vb, own, axis=2, keepdims=False)
        s_own = jnp.einsum('bhqd,bhkd->bhqk', qc, k_own).astype(jnp.float32) * scale
        s_own = jnp.where(kpos[None, :] <= qpos[:, None], s_own, -jnp.inf)
        if n_sel == 0:
            p = jax.nn.softmax(s_own, axis=-1).astype(v.dtype)
            return jnp.einsum('bhqk,bhkd->bhqd', p, v_own)
        gate = jnp.einsum('bhqd,bhnd->bhqn', qc, kbar)
        gate = jnp.where(jnp.arange(nb) < own, gate, -jnp.inf)
        _, idx = lax.top_k(gate, n_sel)
        valid = idx < own
        k_sel = gather(kb, idx)
        v_sel = gather(vb, idx)
        s_sel = jnp.einsum('bhqd,bhqnkd->bhqnk', qc, k_sel).astype(jnp.float32) * scale
        s_sel = jnp.where(valid[..., None], s_sel, -jnp.inf)
        s_sel = s_sel.reshape(bsz, h, MOBA_QCHUNK, n_sel * L)
        p = jax.nn.softmax(jnp.concatenate([s_sel, s_own], axis=-1), axis=-1).astype(v.dtype)
        p_sel = p[..., :n_sel * L].reshape(bsz, h, MOBA_QCHUNK, n_sel, L)
        p_own = p[..., n_sel * L:]
        return (jnp.einsum('bhqnk,bhqnkd->bhqd', p_sel, v_sel)
                + jnp.einsum('bhqk,bhkd->bhqd', p_own, v_own))

    out = lax.map(one_chunk, jnp.arange(sp // MOBA_QCHUNK))
    out = out.transpose(1, 0, 3, 2, 4).reshape(bsz, sp, h * dh)
    return out[:, :s]


def memory_cross_attention(q_m, mem_n, w_mem_kv, memq_gain, memk_gain):
    bsz, s, _ = q_m.shape
    m = mem_n.shape[1]
    q = rmsnorm(q_m.reshape(bsz, s, MEM_HEADS, MEM_HEAD_DIM), memq_gain)
    k_m, v_m = jnp.split(mem_n @ w_mem_kv, 2, axis=-1)
    k = rmsnorm(k_m.reshape(bsz, m, MEM_HEADS, MEM_HEAD_DIM), memk_gain)
    v = v_m.reshape(bsz, m, MEM_HEADS, MEM_HEAD_DIM)
    sc = jnp.einsum('bshd,bmhd->bhsm', q, k).astype(jnp.float32) * (MEM_HEAD_DIM ** -0.5)
    p = jax.nn.softmax(sc, axis=-1).astype(v.dtype)
    o = jnp.einsum('bhsm,bmhd->bshd', p, v)
    return o.reshape(bsz, s, MEM_WIDTH)


def setup_inputs(seed: int = 0) -> dict:
    key = jax.random.key(seed)
    ks = jax.random.split(key, 24)
    nrm = lambda k, shape, scale: jax.random.normal(k, shape, jnp.float32) * scale
    D = D_MODEL
    return {
        "x": nrm(ks[0], (BATCH, SEQ, D), 1.0),
        "mem": nrm(ks[1], (BATCH, MEM_LEN, D), 1.0),
        "g_mix": 1.0 + nrm(ks[2], (D,), 0.02),
        "w_in": nrm(ks[3], (D, IN_COLS), D ** -0.5),
        "b_gate": nrm(ks[4], (N_BRANCH * D,), 0.02),
        "conv_w": nrm(ks[5], (CONV_WIDTH, CONV_CH), CONV_WIDTH ** -0.5),
        "conv_b": nrm(ks[6], (CONV_CH,), 0.02),
        "moba_q_gain": 1.0 + nrm(ks[7], (MOBA_HEAD_DIM,), 0.02),
        "moba_k_gain": 1.0 + nrm(ks[8], (MOBA_HEAD_DIM,), 0.02),
        "g_mem": 1.0 + nrm(ks[9], (D,), 0.02),
        "w_mem_kv": nrm(ks[10], (D, 2 * MEM_WIDTH), D ** -0.5),
        "memq_gain": 1.0 + nrm(ks[11], (MEM_HEAD_DIM,), 0.02),
        "memk_gain": 1.0 + nrm(ks[12], (MEM_HEAD_DIM,), 0.02),
        "w_br_conv": nrm(ks[13], (CONV_CH, D), CONV_CH ** -0.5),
        "w_br_moba": nrm(ks[14], (MOBA_WIDTH, D), MOBA_WIDTH ** -0.5),
        "w_br_mem": nrm(ks[15], (MEM_WIDTH, D), MEM_WIDTH ** -0.5),
        "w_o": nrm(ks[16], (D, D), D ** -0.5),
        "g_ffn": 1.0 + nrm(ks[17], (D,), 0.02),
        "w_up": nrm(ks[18], (D, 2 * D_FF), D ** -0.5),
        "ffn_conv_w": nrm(ks[19], (FFN_CONV_WIDTH, D_FF), FFN_CONV_WIDTH ** -0.5),
        "ffn_conv_b": nrm(ks[20], (D_FF,), 0.02),
        "w_down": nrm(ks[21], (D_FF, D), D_FF ** -0.5),
    }


def reference(x, mem, g_mix, w_in, b_gate, conv_w, conv_b, moba_q_gain, moba_k_gain,
              g_mem, w_mem_kv, memq_gain, memk_gain, w_br_conv, w_br_moba, w_br_mem,
              w_o, g_ffn, w_up, ffn_conv_w, ffn_conv_b, w_down):
    bsz, s, D = x.shape
    mem_n = rmsnorm(mem, g_mem)
    for _ in range(DEPTH):
        h = rmsnorm(x, g_mix)
        proj = h @ w_in
        cuts = list(np.cumsum(IN_SPLITS)[:-1])
        (c_b, c_c, c_x, q, k, v, q_m, g1, g2, g3) = jnp.split(proj, cuts, axis=-1)
        bg1, bg2, bg3 = jnp.split(b_gate, N_BRANCH)

        y_conv = c_b * causal_dwconv3(c_c * c_x, conv_w, conv_b)

        qh = rmsnorm(q.reshape(bsz, s, MOBA_HEADS, MOBA_HEAD_DIM), moba_q_gain)
        kh = rmsnorm(k.reshape(bsz, s, MOBA_HEADS, MOBA_HEAD_DIM), moba_k_gain)
        vh = v.reshape(bsz, s, MOBA_HEADS, MOBA_HEAD_DIM)
        y_moba = moba_attention(qh, kh, vh)

        y_mem = memory_cross_attention(q_m, mem_n, w_mem_kv, memq_gain, memk_gain)

        merged = (jax.nn.sigmoid(g1 + bg1) * (y_conv @ w_br_conv)
                  + jax.nn.sigmoid(g2 + bg2) * (y_moba @ w_br_moba)
                  + jax.nn.sigmoid(g3 + bg3) * (y_mem @ w_br_mem))
        x = x + merged @ w_o

        h2 = rmsnorm(x, g_ffn)
        a, b = jnp.split(h2 @ w_up, 2, axis=-1)
        a = causal_dwconv3(a, ffn_conv_w, ffn_conv_b)
        x = x + (jax.nn.silu(a) * b) @ w_down
    return x
```

```python
import numpy as np
from contextlib import ExitStack
import concourse.bass as bass
import concourse.mybir as mybir
from concourse.bass_utils import run_bass_kernel_spmd

F32 = mybir.dt.float32
BF16 = mybir.dt.bfloat16
AF = mybir.ActivationFunctionType
ALU = mybir.AluOpType
AX = mybir.AxisListType

D = 1024
T = 512
NST = 4
IN_COLS = 6656
D_FF = 2816
NCH_FF = 22
MEM_LEN = 256
EPS = 1e-6
NSLOT = 5
NEG = -30000.0
SYNC_SAME = True


class Eng:
    def __init__(self, name, handle, sem, step=1, inorder=False):
        self.name, self.h, self.sem, self.step, self.inorder = name, handle, sem, step, inorder
        self.cnt = 0
        self.waited = {}


class Buf:
    __slots__ = ("name", "w", "r")

    def __init__(self, name):
        self.name, self.w, self.r = name, None, {}


class K:
    def __init__(self, nc, es):
        self.nc = nc
        self.es = es
        self.engs = {}
        for nm, h, ino in (("pe", nc.tensor, True), ("act", nc.scalar, False), ("dve", nc.vector, False),
                           ("pool", nc.gpsimd, False), ("sp", nc.sync, False)):
            self.engs[nm] = Eng(nm, h, es.enter_context(nc.semaphore("s_" + nm)), 1, ino)
        self.chans = {}

    def chan(self, name):
        if name not in self.chans:
            self.chans[name] = Eng(name, None, self.es.enter_context(self.nc.semaphore("c_" + name)), 16)
        return self.chans[name]

    def _waits(self, eng, reads, writes):
        deps = {}

        def add(e, c):
            if deps.get(e, 0) < c:
                deps[e] = c
        for b in reads:
            if b.w is not None:
                add(*b.w)
        for b in writes:
            if b.w is not None:
                add(*b.w)
            for e, c in b.r.items():
                add(e, c)
        for e, c in deps.items():
            if e is eng and (eng.inorder or not SYNC_SAME):
                continue
            if eng.waited.get(e, 0) < c:
                eng.h.wait_ge(e.sem, c)
                eng.waited[e] = c

    def _mark(self, me_e, me_c, reads, writes):
        for b in reads:
            if b.r.get(me_e, 0) < me_c:
                b.r[me_e] = me_c
        for b in writes:
            b.w = (me_e, me_c)
            b.r = {}

    def op(self, en, fn, reads=(), writes=()):
        eng = self.engs[en]
        self._waits(eng, reads, writes)
        ins = fn()
        eng.cnt += 1
        ins.then_inc(eng.sem, 1)
        self._mark(eng, eng.cnt, reads, writes)

    def dma(self, qn, chname, fn, reads=(), writes=()):
        q = self.engs[qn]
        ch = self.chan(chname)
        self._waits(q, reads, writes)
        ins = fn()
        ch.cnt += 16
        ins.then_inc(ch.sem, 16)
        self._mark(ch, ch.cnt, reads, writes)

    def fence(self, bufs):
        snap = {}
        for e in list(self.engs.values()) + list(self.chans.values()):
            if e.cnt > 0:
                snap[e] = e.cnt
        for b in bufs:
            for e, c in snap.items():
                if b.r.get(e, 0) < c:
                    b.r[e] = c

    def final_wait(self, qn):
        q = self.engs[qn]
        for e in list(self.engs.values()) + list(self.chans.values()):
            if e is q or e.cnt == 0:
                continue
            if q.waited.get(e, 0) < e.cnt:
                q.h.wait_ge(e.sem, e.cnt)
                q.waited[e] = e.cnt


def build_nc(NSEQ, NT):
    S = NT * T
    NBLK = 16
    NCHK = NT * 4
    nc = bass.Bass("TRN2", target_bir_lowering=False)
    es = ExitStack()
    k = K(nc, es)
    es.enter_context(nc.allow_low_precision("bf16 matmul operands by design; reductions accumulate in fp32 internally"))

    def din(name, shape):
        return nc.dram_tensor(name, list(shape), F32, kind="ExternalInput").ap()

    x_d = din("x", (NSEQ, S, D))
    mem_d = din("mem", (NSEQ, MEM_LEN, D))
    g_mix_d = din("g_mix", (D,))
    w_in_d = din("w_in", (D, IN_COLS))
    b_gate_d = din("b_gate", (3 * D,))
    conv_w_d = din("conv_w", (3, 512))
    conv_b_d = din("conv_b", (512,))
    qg_d = din("moba_q_gain", (64,))
    kg_d = din("moba_k_gain", (64,))
    g_mem_d = din("g_mem", (D,))
    w_mkv_d = din("w_mem_kv", (D, 1024))
    mqg_d = din("memq_gain", (128,))
    mkg_d = din("memk_gain", (128,))
    w_brc_d = din("w_br_conv", (512, D))
    w_brm_d = din("w_br_moba", (512, D))
    w_bre_d = din("w_br_mem", (512, D))
    w_o_d = din("w_o", (D, D))
    g_ffn_d = din("g_ffn", (D,))
    w_up_d = din("w_up", (D, 2 * D_FF))
    fcw_d = din("ffn_conv_w", (3, D_FF))
    fcb_d = din("ffn_conv_b", (D_FF,))
    w_dn_d = din("w_down", (D_FF, D))
    out_d = nc.dram_tensor("out", [NSEQ, S, D], F32, kind="ExternalOutput").ap()

    def dscr(name, shape):
        return nc.dram_tensor(name, list(shape), BF16, kind="Internal").ap()

    w_in_b = dscr("w_in_b", (D, IN_COLS))
    w_mkv_b = dscr("w_mkv_b", (D, 1024))
    w_br_b = [dscr("w_brc_b", (512, D)), dscr("w_brm_b", (512, D)), dscr("w_bre_b", (512, D))]
    w_o_b = dscr("w_o_b", (D, D))
    w_up_b = dscr("w_up_b", (D, 2 * D_FF))
    w_dn_b = dscr("w_dn_b", (D_FF, D))

    def sb(name, shape, dt):
        return es.enter_context(nc.sbuf_tensor(name, list(shape), dt))

    T_X = sb("T_X", [128, NST, 2048], BF16)
    T_hT = sb("T_hT", [128, 8, T], BF16)
    T_KT = sb("T_KT", [128, 8, S], BF16)
    T_V = sb("T_V", [128, NCHK, 8, 66], BF16)
    T_W = sb("T_W", [128, NSLOT, 2048], BF16)
    T_M = sb("T_M", [128, 23048], BF16)
    ident = sb("ident", [128, 128], BF16)
    identf = sb("identf", [128, 128], F32)
    tri = sb("tri", [128, 128], BF16)
    zer = sb("zer", [128, 260], BF16)
    pen = sb("pen", [128, 16, 16], F32)
    colsA = sb("colsA", [128, 128], F32)
    colsB = sb("colsB", [128, 128], F32)
    g2 = sb("g2", [128, 64], F32)
    g2k = sb("g2k", [128, 64], F32)
    mg2 = sb("mg2", [128, 128], F32)
    mg2k = sb("mg2k", [128, 128], F32)
    eps_c = sb("eps_c", [128, 1], F32)
    KmT = sb("KmT", [128, 4, MEM_LEN], BF16)
    Vm = sb("Vm", [128, 2, 4, 130], BF16)
    kbarT = sb("kbarT", [128, 8, 16], BF16)
    uh = sb("uh", [128, 4, 2], F32)
    ah = sb("ah", [128, NCH_FF, 2], F32)
    st_ss = sb("st_ss", [128, 8], F32)
    st_rms = sb("st_rms", [128, 8], F32)
    st_rstd = sb("st_rstd", [128, 8], F32)
    n_ss = [sb(f"n_ss{i}", [128, 8], F32) for i in range(2)]
    n_rms = [sb(f"n_rms{i}", [128, 8], F32) for i in range(2)]
    n_rstd = [sb(f"n_rstd{i}", [128, 8], F32) for i in range(2)]
    gmS = [sb(f"gm{i}", [128, 8, 16], F32) for i in range(1)]
    mxS = [sb(f"mx{i}", [128, 8, 8], F32) for i in range(1)]
    thrS = [sb(f"thr{i}", [128, 8], F32) for i in range(1)]
    selS = [sb(f"sel{i}", [128, 8, 16], F32) for i in range(1)]
    nmS = [sb(f"nm{i}", [128, 8, 64], BF16) for i in range(4)]
    rcS = [sb(f"rc{i}", [128, 4], F32) for i in range(2)]
    PS = [es.enter_context(nc.psum_tensor(f"ps{i}", [128, 512], F32)) for i in range(8)]

    B = {}

    def bf(name):
        if name not in B:
            B[name] = Buf(name)
        return B[name]

    bX = [bf(f"X{i}") for i in range(NST)]
    bhT = [bf(f"hT{i}") for i in range(NST)]
    bKT = [bf(f"KT{h}") for h in range(8)]
    bKT1 = [bf(f"KT1_{h}") for h in range(8)]
    bV = bf("V")
    bW = [bf(f"W{i}") for i in range(NSLOT)]
    bPS = [bf(f"PS{i}") for i in range(8)]
    bC = bf("consts")
    bC2 = bf("consts2")
    bRows, bId, bG = bf("rows"), bf("idc"), bf("gains")
    bKm, bVm, bkbar, buh, bah = bf("KmT"), bf("Vm"), bf("kbar"), bf("uh"), bf("ah")
    bstat = [bf(f"stat{i}") for i in range(NST)]
    bn = [bf("nstat0"), bf("nstat1")]
    bgate = [bf("gate0"), bf("gate1")]
    bnm = [bf(f"nm{i}") for i in range(4)]
    brc = [bf("rc0"), bf("rc1")]
    O_QT, O_QM, O_PT, O_Y2, O_YC, O_TMP, O_ACC, O_YU = 0, 4096, 6144, 7680, 11776, 13824, 17920, 22016
    bU = bf("u")
    bQT = [bf(f"QT{h}") for h in range(8)]
    bQm = bf("QmT")
    bPT = [bf(f"PT{i}") for i in range(3)]
    bY2 = [bf(f"Y2_{i}") for i in range(4)]
    bYC = bf("yTc")
    bTMP = [bf(f"tmp{i}") for i in range(4)]
    bACC = bf("macc")
    bACT = bf("actT")
    bASB = [bf("asb0"), bf("asb1")]
    mixer_bufs = bQT + [bQm] + bPT + bY2 + [bYC] + bTMP + [bACC]
    ffn_bufs = [bACT] + bASB + bTMP

    def QT(h):
        return T_M[0:80, O_QT + h * T: O_QT + (h + 1) * T]

    def QmTv(h):
        return T_M[:, O_QM + h * T: O_QM + (h + 1) * T]

    def PTv(i):
        return T_M[:, O_PT + i * T: O_PT + (i + 1) * T]

    def hpre(st):
        return T_M[:, O_Y2 + st * 1024: O_Y2 + (st + 1) * 1024]

    def yTm(c):
        return T_M[:, O_Y2 + c * T: O_Y2 + (c + 1) * T]

    def yTe(c):
        return T_M[:, O_Y2 + 2048 + c * T: O_Y2 + 2048 + (c + 1) * T]

    def yTc(c):
        return T_M[:, O_YC + c * T: O_YC + (c + 1) * T]

    def tmpf(i, n=512, off=0):
        return T_M[:, O_TMP + i * 1024: O_TMP + (i + 1) * 1024].bitcast(F32)[:, off:off + n]

    def tmpb(i):
        return T_M[:, O_TMP + i * 1024: O_TMP + (i + 1) * 1024]

    def macc(c):
        return T_M[:, O_ACC + c * T: O_ACC + (c + 1) * T]

    def actT(m):
        return T_M[:, m * T:(m + 1) * T]

    def asb(i):
        o = 11264 + i * 1028
        return T_M[:, o:o + 1028].bitcast(F32)

    def xf(st):
        return T_X[:, st, :].bitcast(F32)

    def qkpre(st):
        return T_X[:, st, 0:1024]

    def qmpre(st):
        return T_X[:, st, 1024:1536]

    rot = [0]

    def ps_rot():
        b = rot[0] % 5
        rot[0] += 1
        return b

    accb = [0]

    def ps_acc():
        b = 5 + (accb[0] % 2)
        accb[0] += 1
        return b

    def psb16(b):
        return PS[b][:].bitcast(BF16)

    altc = [0]

    def evac_copy(out, in_, reads, writes):
        altc[0] += 1
        if altc[0] % 2 == 0:
            k.op("act", lambda: nc.scalar.copy(out=out, in_=in_), reads, writes)
        else:
            k.op("dve", lambda: nc.vector.tensor_copy(out=out, in_=in_), reads, writes)

    wslot = [0]

    def wload(src_ap, nkc, srcbuf):
        s = wslot[0] % NSLOT
        wslot[0] += 1
        dst = T_W[:, s, 0:nkc * 256].rearrange("p (a b) -> p a b", a=nkc)
        k.dma("sp", f"w{s}", lambda: nc.sync.dma_start(out=dst, in_=src_ap), [srcbuf], [bW[s]])
        return dst, bW[s]

    bsrc = {n: bf("src_" + n) for n in ("winA", "winB", "winC", "mkv", "br0", "br1", "br2", "wo", "wup", "wdn")}

    def w_in_src(gi):
        c = gi * 256
        return bsrc["winA"] if 1536 <= c < 3584 else (bsrc["winB"] if c < 1536 else bsrc["winC"])

    def w_in_grp(gi):
        return wload(w_in_b[:, gi * 256:(gi + 1) * 256].rearrange("(kc p) n -> p kc n", p=128), 8, w_in_src(gi))

    def w_mkv_grp(gi):
        return wload(w_mkv_b[:, gi * 256:(gi + 1) * 256].rearrange("(kc p) n -> p kc n", p=128), 8, bsrc["mkv"])

    def w_br_grp(i, gi):
        return wload(w_br_b[i][:, gi * 256:(gi + 1) * 256].rearrange("(kc p) n -> p kc n", p=128), 4, bsrc[f"br{i}"])

    def w_o_grp(gi):
        return wload(w_o_b[:, gi * 256:(gi + 1) * 256].rearrange("(kc p) n -> p kc n", p=128), 8, bsrc["wo"])

    def w_up_grp(gi):
        return wload(w_up_b[:, gi * 256:(gi + 1) * 256].rearrange("(kc p) n -> p kc n", p=128), 8, bsrc["wup"])

    def w_dn_grp(cg, m0, cnt):
        return wload(w_dn_b[m0 * 128:(m0 + cnt) * 128, cg * 256:(cg + 1) * 256].rearrange("(m p) n -> p m n", p=128),
                     cnt, bsrc["wdn"])

    def cast(dst, src, rows, name, nsplit, after=()):
        step = rows // nsplit
        for i in range(nsplit):
            k.dma("pool", "cast_" + name,
                  lambda i=i: nc.gpsimd.dma_start(out=dst[i * step:(i + 1) * step, :], in_=src[i * step:(i + 1) * step, :]),
                  list(after), [bsrc[name]])

    def cast_win(names, after=()):
        for name, c0, c1 in (("winA", 1536, 3584), ("winB", 0, 1536), ("winC", 3584, IN_COLS)):
            if name not in names:
                continue
            for i in range(4):
                k.dma("pool", "cast_" + name,
                      lambda i=i, c0=c0, c1=c1: nc.gpsimd.dma_start(out=w_in_b[i * 256:(i + 1) * 256, c0:c1], in_=w_in_d[i * 256:(i + 1) * 256, c0:c1]),
                      list(after), [bsrc[name]])

    casts2_pending = [True]

    def casts2():
        if not casts2_pending[0]:
            return
        casts2_pending[0] = False
        cast_win(("winB",), after=bX)
        for i, wd in enumerate((w_brc_d, w_brm_d, w_bre_d)):
            cast(w_br_b[i], wd, 512, f"br{i}", 1, after=bX)
        cast_win(("winC",), after=bX)

    bsrc["wo"].w = bsrc["wup"].w = bsrc["wdn"].w = None
    deferred = [True]

    def deferred_casts():
        if not deferred[0]:
            return
        deferred[0] = False
        cast(w_o_b, w_o_d, D, "wo", 2, after=[bsrc["winC"]])
        cast(w_up_b, w_up_d, D, "wup", 8, after=[bsrc["winC"]])
        cast(w_dn_b, w_dn_d, D_FF, "wdn", 4, after=[bsrc["winC"]])

    pool, dve, act, pe = nc.gpsimd, nc.vector, nc.scalar, nc.tensor
    for st in range(2):
        k.dma("pool", f"xld{st}", lambda st=st: nc.gpsimd.dma_start(out=T_X[:, st, :].bitcast(F32), in_=mem_d[0, st * 128:(st + 1) * 128, :]), [], [bX[st]])
    rowsA = T_M[:, O_TMP:O_TMP + 256].bitcast(F32)
    rowsB = T_M[:, O_TMP + 1024:O_TMP + 1280].bitcast(F32)
    k.op("pool", lambda: pool.memset(rowsA, 0.0), [], [bRows, bTMP[0]])
    k.op("pool", lambda: pool.memset(rowsB, 0.0), [], [bRows, bTMP[1]])

    def rowload(dst_t, r0, src, n):
        k.dma("sp", "cst", lambda: nc.sync.dma_start(out=dst_t[r0:r0 + n, :], in_=src.rearrange("(r c) -> r c", c=128)), [], [bRows, bTMP[0], bTMP[1]])

    rowload(rowsA, 0, g_mix_d, 8)
    rowload(rowsA, 8, g_ffn_d, 8)
    rowload(rowsA, 16, g_mem_d, 8)
    rowload(rowsA, 24, b_gate_d, 24)
    rowload(rowsA, 48, conv_w_d.rearrange("a b -> (a b)"), 12)
    rowload(rowsA, 60, conv_b_d, 4)
    rowload(rowsA, 64, fcb_d, 22)
    rowload(rowsB, 0, fcw_d.rearrange("a b -> (a b)"), 66)
    CA = dict(gmix=0, gffn=8, gmem=16, bgate=24, convw=48, convb=60, fcb=64)
    for dstt, src, n in ((g2, qg_d, 64), (g2k, kg_d, 64), (mg2, mqg_d, 128), (mg2k, mkg_d, 128)):
        k.dma("sp", "cst", lambda dstt=dstt, src=src, n=n: nc.sync.dma_start(out=dstt[:, 0:n], in_=src.partition_broadcast(128)), [], [bG])
    k.op("pool", lambda: pool.memset(eps_c[:], EPS), [], [bId])
    k.op("pool", lambda: pool.memset(ident[:], 1.0), [], [bId])
    k.op("pool", lambda: pool.affine_select(out=ident[:], in_=ident[:], pattern=[[-1, 128]], compare_op=ALU.is_equal,
                                             fill=0.0, base=0, channel_multiplier=1), [bId], [bId])
    k.op("pool", lambda: pool.memset(identf[:], 1.0), [], [bId])
    k.op("pool", lambda: pool.affine_select(out=identf[:], in_=identf[:], pattern=[[-1, 128]], compare_op=ALU.is_equal,
                                             fill=0.0, base=0, channel_multiplier=1), [bId], [bId])
    k.op("pool", lambda: pool.memset(Vm[:, :, :, 128:130], 1.0), [], [bVm])
    k.op("pool", lambda: pool.memset(zer[:], 0.0), [], [bC2])
    k.op("pool", lambda: pool.memset(kbarT[:], 0.0), [], [bkbar])
    k.op("pool", lambda: pool.memset(T_V[:, :, :, 64:66], 1.0), [], [bV])
    for i in range(4):
        k.op("pool", lambda i=i: pool.memset(nmS[i][:], 0.0), [], [bnm[i]])
    cast(w_mkv_b, w_mkv_d, D, "mkv", 2)
    cast_win(("winA",))
    for rows_t, cols_t, nr in ((rowsA, colsA, 86), (rowsB, colsB, 66)):
        b = ps_rot()
        k.op("pe", lambda rows_t=rows_t, b=b, nr=nr: pe.transpose(out=PS[b][:, 0:nr], in_=rows_t[0:nr, :], identity=identf[0:nr, 0:nr]),
             [bRows, bId, bTMP[0], bTMP[1]], [bPS[b]])
        k.op("dve", lambda cols_t=cols_t, b=b, nr=nr: dve.tensor_copy(out=cols_t[:, 0:nr], in_=PS[b][:, 0:nr]), [bPS[b]], [bC])
    k.op("dve", lambda: dve.tensor_tensor(out=g2[:], in0=g2[:], in1=g2k[:], op=ALU.mult), [bG], [bG])
    k.op("dve", lambda: dve.tensor_tensor(out=mg2[:], in0=mg2[:], in1=mg2k[:], op=ALU.mult), [bG], [bG])
    k.op("pool", lambda: pool.affine_select(out=tri[:], in_=zer[:, 0:128], pattern=[[1, 128]], compare_op=ALU.is_ge,
                                             fill=NEG, base=0, channel_multiplier=-1), [bC2], [bC2])
    k.op("pool", lambda: pool.memset(pen[:], 0.0), [], [bC2])
    k.op("pool", lambda: pool.affine_select(out=pen[:], in_=pen[:], pattern=[[1, 16], [-1, 16]], compare_op=ALU.is_ge,
                                             fill=-1e30, base=-1, channel_multiplier=0), [bC2], [bC2])
    k.op("pool", lambda: pool.memset(T_KT[64:80, 0, :], 1.0), [], [bKT1[0]])
    k.op("pool", lambda: pool.affine_select(out=T_KT[64:80, 0, :], in_=T_KT[64:80, 0, :], pattern=[[1, S]],
                                             compare_op=ALU.is_ge, fill=0.0, base=0, channel_multiplier=-256), [], [bKT1[0]])
    k.op("pool", lambda: pool.affine_select(out=T_KT[64:80, 0, :], in_=T_KT[64:80, 0, :], pattern=[[-1, S]],
                                             compare_op=ALU.is_ge, fill=0.0, base=255, channel_multiplier=256), [], [bKT1[0]])
    late = [True]

    def late_consts():
        if not late[0]:
            return
        late[0] = False
        for h in range(1, 8):
            k.op("dve", lambda h=h: dve.tensor_copy(out=T_KT[64:80, h, :], in_=T_KT[64:80, 0, :]), [bKT1[0]], [bKT1[h]])

    def normA(st, xbuf, src=None, junk=None, junkb=None, hp=None, hpb=None):
        src = xf(st) if src is None else src
        junk = tmpb(2 + st % 2) if junk is None else junk
        junkb = [bTMP[2 + st % 2]] if junkb is None else junkb
        hp = hpre(st) if hp is None else hp
        hpb = [bY2[st]] if hpb is None else hpb
        k.op("act", lambda: act.activation(out=junk, in_=src, func=AF.Square, accum_out=st_ss[:, st:st + 1]),
             list(xbuf), junkb + [bstat[st]])
        k.op("act", lambda: act.activation(out=st_rms[:, st:st + 1], in_=st_ss[:, st:st + 1], func=AF.Sqrt, bias=eps_c[:], scale=1.0 / D),
             [bstat[st], bC], [bstat[st]])
        k.op("dve", lambda: dve.reciprocal(out=st_rstd[:, st:st + 1], in_=st_rms[:, st:st + 1]), [bstat[st]], [bstat[st]])
        k.op("act", lambda: act.activation(out=hp, in_=src, func=AF.Identity, scale=st_rstd[:, st:st + 1]),
             list(xbuf) + [bstat[st]], hpb)

    def normB(st, gcol0, hp=None, hpb=None, bank=None):
        hp = hpre(st) if hp is None else hp
        hpb = [bY2[st]] if hpb is None else hpb
        b = ps_rot() if bank is None else bank
        for kc in range(8):
            k.op("pe", lambda kc=kc: pe.transpose(out=psb16(b)[:, kc * 128:(kc + 1) * 128], in_=hp[:, kc * 128:(kc + 1) * 128],
                                                  identity=ident[:]), hpb + [bC], [bPS[b]])
        k.op("dve", lambda: dve.tensor_tensor(out=T_hT[:, :, st * 128:(st + 1) * 128], in0=psb16(b).rearrange("p (a b) -> p a b", a=8),
                                              in1=colsA[:, gcol0:gcol0 + 8].unsqueeze(2).to_broadcast([128, 8, 128]), op=ALU.mult),
             [bPS[b], bC], [bhT[st]])

    def norm_to_hT(nst, gcol0, xbufs):
        for st in range(nst):
            normA(st, [xbufs[st]])
            if st >= 1:
                normB(st - 1, gcol0)
        normB(nst - 1, gcol0)

    def xs(i):
        return T_M[:, O_TMP + i * 2048:O_TMP + (i + 1) * 2048].bitcast(F32)

    def early_A(seq, j1, st):
        i = st % 2
        xb = [bTMP[2 * i], bTMP[2 * i + 1]]
        k.dma("pool", f"xs{i}", lambda: nc.gpsimd.dma_start(out=xs(i), in_=x_d[seq, j1 * T + st * 128:j1 * T + (st + 1) * 128, :]), [], xb)
        normA(st, xb, src=xs(i), junk=T_M[:, 11264:12288], junkb=bASB, hp=T_M[:, O_ACC + st * 1024:O_ACC + (st + 1) * 1024], hpb=[bACC])

    def early_B(st):
        normB(st, CA["gmix"], hp=T_M[:, O_ACC + st * 1024:O_ACC + (st + 1) * 1024], hpb=[bACC], bank=7)

    def tok_proj(st, slots, lhs_of_kc, lhsbufs):
        b = ps_rot()
        for half, (wv, wb) in enumerate(slots):
            for kc in range(8):
                k.op("pe", lambda kc=kc, wv=wv, half=half, b=b: pe.matmul(
                    PS[b][:, half * 256:(half + 1) * 256], lhsT=lhs_of_kc(kc, st), rhs=wv[:, kc, :], start=(kc == 0), stop=(kc == 7)),
                    lhsbufs + [wb], [bPS[b]])
        return b

    def hT_lhs(kc, st):
        return T_hT[:, kc, st * 128:(st + 1) * 128]

    nsi = [0]

    def head_norm(b, nh, hd, out_ap, outbufs, gain_ap=None):
        i = nsi[0] % 2
        nsi[0] += 1
        sq = tmpf(i)
        k.op("act", lambda: act.activation(out=sq, in_=PS[b][:], func=AF.Square), [bPS[b]], [bTMP[i]])
        k.op("dve", lambda: dve.tensor_reduce(out=n_ss[i][:, 0:nh], in_=sq.rearrange("p (a b) -> p a b", a=nh), axis=AX.X, op=ALU.add),
             [bTMP[i]], [bn[i]])
        k.op("act", lambda: act.activation(out=n_rms[i][:, 0:nh], in_=n_ss[i][:, 0:nh], func=AF.Sqrt, bias=eps_c[:], scale=1.0 / hd),
             [bn[i], bC], [bn[i]])
        k.op("dve", lambda: dve.reciprocal(out=n_rstd[i][:, 0:nh], in_=n_rms[i][:, 0:nh]), [bn[i]], [bn[i]])
        rb = n_rstd[i][:, 0:nh].unsqueeze(2).to_broadcast([128, nh, hd])
        pv = PS[b][:].rearrange("p (a b) -> p a b", a=nh)
        if gain_ap is None:
            k.op("dve", lambda: dve.tensor_tensor(out=out_ap.rearrange("p (a b) -> p a b", a=nh), in0=pv, in1=rb, op=ALU.mult),
                 [bPS[b], bn[i]], outbufs)
        else:
            k.op("dve", lambda: dve.tensor_tensor(out=sq.rearrange("p (a b) -> p a b", a=nh), in0=pv, in1=rb, op=ALU.mult),
                 [bPS[b], bn[i]], [bTMP[i]])
            k.op("dve", lambda: dve.tensor_tensor(out=out_ap.rearrange("p (a b) -> p a b", a=nh), in0=sq.rearrange("p (a b) -> p a b", a=nh),
                                                  in1=gain_ap, op=ALU.mult), [bTMP[i], bG], outbufs)

    def mem_prologue(seq):
        for st in range(2):
            if seq == 0:
                break
            k.dma("pool", f"xld{st}", lambda st=st: nc.gpsimd.dma_start(out=xf(st), in_=mem_d[seq, st * 128:(st + 1) * 128, :]), [], [bX[st]])
        norm_to_hT(2, CA["gmem"], bX)
        ks = [w_mkv_grp(0), w_mkv_grp(1)]
        for st in range(2):
            b = tok_proj(st, ks, hT_lhs, [bhT[st]])
            head_norm(b, 4, 128, T_X[:, 2 + st, 0:512], [bX[2 + st]])
        vs = [w_mkv_grp(2), w_mkv_grp(3)]
        for st in range(2):
            b = tok_proj(st, vs, hT_lhs, [bhT[st]])
            k.op("act", lambda st=st, b=b: act.copy(out=Vm[:, st, :, 0:128], in_=PS[b][:].rearrange("p (a b) -> p a b", a=4)),
                 [bPS[b]], [bVm])
        for hp in range(2):
            b = ps_rot()
            for h in (2 * hp, 2 * hp + 1):
                for st in range(2):
                    k.op("pe", lambda h=h, st=st, b=b: pe.transpose(
                        out=psb16(b)[:, (h % 2) * 512 + st * 128:(h % 2) * 512 + (st + 1) * 128],
                        in_=T_X[:, 2 + st, h * 128:(h + 1) * 128], identity=ident[:]), [bX[2 + st], bC], [bPS[b]])
            for h in (2 * hp, 2 * hp + 1):
                evac_copy(KmT[:, h, :], psb16(b)[:, (h % 2) * 512:(h % 2) * 512 + 256], [bPS[b]], [bKm])
        k.op("pool", lambda: pool.memset(uh[:], 0.0), [], [buh])
        k.op("pool", lambda: pool.memset(ah[:], 0.0), [], [bah])

    def tile(seq, j, p0_done, do_early):
        k.fence(mixer_bufs)
        if not p0_done:
            for st in range(NST):
                if seq == 0:
                    k.dma("sp", f"xld{st}", lambda st=st: nc.sync.dma_start(out=xf(st), in_=x_d[seq, j * T + st * 128:j * T + (st + 1) * 128, :]), [], [bX[st]])
                else:
                    k.dma("pool", f"xld{st}", lambda st=st: nc.gpsimd.dma_start(out=xf(st), in_=x_d[seq, j * T + st * 128:j * T + (st + 1) * 128, :]), [], [bX[st]])
            casts2()
            norm_to_hT(NST, CA["gmix"], bX)
        for which, g0 in ((0, 6), (1, 8)):
            sl = [w_in_grp(g0), w_in_grp(g0 + 1)]
            for st in range(NST):
                b = tok_proj(st, sl, hT_lhs, [bhT[st]])
                outap = qkpre(st)[:, which * 512:(which + 1) * 512]
                if which == 0:
                    head_norm(b, 8, 64, outap, [bX[st]], gain_ap=g2[:].unsqueeze(1).to_broadcast([128, 8, 64]))
                else:
                    head_norm(b, 8, 64, outap, [bX[st]])
        sl = [w_in_grp(10), w_in_grp(11)]
        for st in range(NST):
            b = tok_proj(st, sl, hT_lhs, [bhT[st]])
            k.op("act", lambda st=st, b=b: act.copy(out=T_V[:, j * 4 + st, :, 0:64], in_=PS[b][:].rearrange("p (a b) -> p a b", a=8)),
                 [bPS[b]], [bV])
        sl = [w_in_grp(12), w_in_grp(13)]
        for st in range(NST):
            b = tok_proj(st, sl, hT_lhs, [bhT[st]])
            head_norm(b, 4, 128, qmpre(st), [bX[st]], gain_ap=mg2[:].unsqueeze(1).to_broadcast([128, 4, 128]))
        for which in (0, 1):
            for pp in range(2):
                b = ps_rot()
                for p in (2 * pp, 2 * pp + 1):
                    for st in range(NST):
                        k.op("pe", lambda p=p, st=st, b=b: pe.transpose(
                            out=psb16(b)[:, (p % 2) * 512 + st * 128:(p % 2) * 512 + (st + 1) * 128],
                            in_=qkpre(st)[:, which * 512 + p * 128: which * 512 + (p + 1) * 128], identity=ident[:]),
                            [bX[st], bC], [bPS[b]])
                for p in (2 * pp, 2 * pp + 1):
                    for e in range(2):
                        h = 2 * p + e
                        src = psb16(b)[64 * e:64 * e + 64, (p % 2) * 512:(p % 2) * 512 + 512]
                        if which == 0:
                            evac_copy(QT(h)[0:64, :], src, [bPS[b]], [bQT[h]])
                        else:
                            evac_copy(T_KT[0:64, h, j * T:(j + 1) * T], src, [bPS[b]], [bKT[h]])
                            k.op("dve", lambda h=h: dve.tensor_reduce(
                                out=kbarT[0:64, h, 2 * j:2 * j + 2], in_=T_KT[0:64, h, j * T:(j + 1) * T].rearrange("p (a b) -> p a b", a=2),
                                axis=AX.X, op=ALU.add), [bKT[h]], [bkbar])
        for hp in range(2):
            b = ps_rot()
            for h in (2 * hp, 2 * hp + 1):
                for st in range(NST):
                    k.op("pe", lambda h=h, st=st, b=b: pe.transpose(
                        out=psb16(b)[:, (h % 2) * 512 + st * 128:(h % 2) * 512 + (st + 1) * 128],
                        in_=qmpre(st)[:, h * 128:(h + 1) * 128], identity=ident[:]), [bX[st], bC], [bPS[b]])
            for h in (2 * hp, 2 * hp + 1):
                evac_copy(QmTv(h), psb16(b)[:, (h % 2) * 512:(h % 2) * 512 + 512], [bPS[b]], [bQm])
        cslots = {}

        def conv_chunk(m):
            mh = m // 2
            if m % 2 == 0:
                cslots[mh] = (w_in_grp(2 + mh), w_in_grp(4 + mh), w_in_grp(0 + mh))
            scc, scx, scb = cslots[mh]
            off = (m % 2) * 128
            banks = []
            for (wv, wb) in (scc, scx, scb):
                b = ps_rot()
                banks.append(b)
                for kc in range(8):
                    k.op("pe", lambda kc=kc, wv=wv, b=b: pe.matmul(PS[b][:], lhsT=wv[:, kc, off:off + 128], rhs=T_hT[:, kc, :],
                                                                   start=(kc == 0), stop=(kc == 7)), bhT + [wb], [bPS[b]])
            bcc, bcx, bcb = banks
            u = T_M[:, O_YU:O_YU + 1028].bitcast(F32)
            k.op("act", lambda: act.copy(out=tmpf(2), in_=PS[bcc][:]), [bPS[bcc]], [bTMP[2]])
            k.op("pool", lambda: pool.tensor_copy(out=u[:, 0:2], in_=uh[:, m, :]), [buh], [bU])
            k.op("dve", lambda: dve.tensor_tensor(out=u[:, 2:514], in0=PS[bcx][:], in1=tmpf(2), op=ALU.mult), [bPS[bcx], bTMP[2]], [bU])
            k.op("pool", lambda: pool.tensor_copy(out=uh[:, m, :], in_=u[:, 512:514]), [bU], [buh])
            cw = CA["convw"]
            k.op("act", lambda: act.activation(out=tmpf(0), in_=u[:, 2:514], func=AF.Identity, bias=colsA[:, CA["convb"] + m:CA["convb"] + m + 1],
                                               scale=colsA[:, cw + 8 + m:cw + 8 + m + 1]), [bU, bC], [bTMP[0]])
            k.op("dve", lambda: dve.scalar_tensor_tensor(out=tmpf(1), in0=u[:, 1:513], scalar=colsA[:, cw + 4 + m:cw + 4 + m + 1], in1=tmpf(0),
                                                         op0=ALU.mult, op1=ALU.add), [bU, bC, bTMP[0]], [bTMP[1]])
            k.op("dve", lambda: dve.scalar_tensor_tensor(out=tmpf(0), in0=u[:, 0:512], scalar=colsA[:, cw + m:cw + m + 1], in1=tmpf(1),
                                                         op0=ALU.mult, op1=ALU.add), [bU, bC, bTMP[1]], [bTMP[0]])
            k.op("dve", lambda: dve.tensor_tensor(out=yTc(m), in0=PS[bcb][:], in1=tmpf(0), op=ALU.mult), [bPS[bcb], bTMP[0]], [bYC])

        conv_chunk(0)
        for st in range(NST):
            for h in range(8):
                k.op("pe", lambda: pe.matmul(PS[7][:, st * 128 + h * 16:st * 128 + (h + 1) * 16], lhsT=QT(h)[0:64, st * 128:(st + 1) * 128],
                                             rhs=kbarT[0:64, h, :], start=True, stop=True), [bQT[h], bkbar], [bPS[7]])
        def gate_chain(st):
            own = 2 * j + st // 2
            gm, mx, thr, sel, nm = gmS[0], mxS[0], thrS[0], selS[0], nmS[st]
            k.op("dve", lambda: dve.tensor_tensor(out=gm[:], in0=PS[7][:, st * 128:(st + 1) * 128].rearrange("p (a b) -> p a b", a=8),
                                                  in1=pen[:, own, :].unsqueeze(1).to_broadcast([128, 8, 16]), op=ALU.add), [bPS[7], bC2], [bgate[0]])
            for h in range(8):
                k.op("dve", lambda h=h: dve.max(out=mx[:, h, :], in_=gm[:, h, :]), [bgate[0]], [bgate[0]])
            k.op("dve", lambda: dve.tensor_scalar(out=thr[:], in0=mx[:, :, 2], scalar1=-1e29, scalar2=None, op0=ALU.max), [bgate[0]], [bgate[0]])
            k.op("dve", lambda: dve.tensor_tensor(out=sel[:], in0=gm[:], in1=thr[:].unsqueeze(2).to_broadcast([128, 8, 16]), op=ALU.is_ge),
                 [bgate[0]], [bgate[0]])
            k.op("dve", lambda: dve.tensor_scalar(out=nm[:, :, 0:16], in0=sel[:], scalar1=-NEG, scalar2=NEG, op0=ALU.mult, op1=ALU.add),
                 [bgate[0]], [bnm[st]])
            k.op("dve", lambda: dve.memset(nm[:, :, own:own + 1], 0.0), [], [bnm[st]])

        pti = [0]

        def pipeline(items, LA=2):
            n = len(items)
            rs = [None] * n
            for i in range(min(LA, n)):
                rs[i] = items[i][0]()
            for i in range(n):
                if i + LA < n:
                    rs[i + LA] = items[i + LA][0]()
                items[i][1](rs[i])

        def mem_S(h, mc):
            b = ps_rot()
            k.op("pe", lambda: pe.matmul(PS[b][:], lhsT=KmT[:, h, mc * 128:(mc + 1) * 128], rhs=QmTv(h), start=True, stop=True),
                 [bKm, bQm], [bPS[b]])
            r = pti[0] % 3
            pti[0] += 1
            k.op("act", lambda: act.activation(out=PTv(r), in_=PS[b][:], func=AF.Exp, scale=128 ** -0.5), [bPS[b]], [bPT[r]])
            return r

        def mem_PV(h, mc, r):
            ab = [5, 6]
            if mc == 0:
                for a in ab:
                    k.op("pe", lambda a=a: pe.matmul(PS[a][:, 0:258], lhsT=zer[:, 0:128], rhs=zer[:, 0:258], start=True, stop=True), [bC2], [bPS[a]])
            for st in range(NST):
                a = ab[st // 2]
                k.op("pe", lambda st=st, a=a: pe.matmul(PS[a][:, (st % 2) * 129:(st % 2) * 129 + 129], lhsT=PTv(r)[:, st * 128:(st + 1) * 128],
                                                        rhs=Vm[:, mc, h, 0:129], start=False, stop=True, skip_group_check=True),
                     [bPT[r], bVm], [bPS[a]])
            if mc == 1:
                for ai, a in enumerate(ab):
                    av = PS[a][:, 0:258].rearrange("p (a b) -> p a b", a=2)
                    k.op("dve", lambda av=av, ai=ai: dve.reciprocal(out=rcS[ai][:, 0:2], in_=av[:, :, 128]), [bPS[a]], [brc[ai]])
                    k.op("dve", lambda av=av, ai=ai: dve.tensor_tensor(
                        out=T_X[:, 2 * ai:2 * ai + 2, 512 + h * 128:512 + (h + 1) * 128], in0=av[:, :, 0:128],
                        in1=rcS[ai][:, 0:2].unsqueeze(2).to_broadcast([128, 2, 128]), op=ALU.mult),
                        [bPS[a], brc[ai]], [bX[2 * ai], bX[2 * ai + 1]])

        for h in range(4):
            items = []
            for mc in range(2):
                items.append((lambda h=h, mc=mc: mem_S(h, mc), lambda r, h=h, mc=mc: mem_PV(h, mc, r)))
            pipeline(items)
            gate_chain(h)
            if h < 3:
                conv_chunk(h + 1)

        deferred_casts()
        late_consts()
        nmb = [ps_rot(), ps_rot()]
        for st in range(NST):
            nmf = nmS[st][:].rearrange("p a b -> p (a b)")
            for p in range(4):
                b = nmb[p // 2]
                k.op("pe", lambda p=p, b=b: pe.transpose(out=psb16(b)[:, (p % 2) * 512 + st * 128:(p % 2) * 512 + (st + 1) * 128],
                                                         in_=nmf[:, p * 128:(p + 1) * 128], identity=ident[:]), [bnm[st], bC], [bPS[b]])
        for p in range(4):
            b = nmb[p // 2]
            for e in range(2):
                h = 2 * p + e
                evac_copy(QT(h)[64:80, :], psb16(b)[64 * e:64 * e + 16, (p % 2) * 512:(p % 2) * 512 + 512], [bPS[b]], [bQT[h]])

        nchunk = 4 * j + 4
        hstate = {}

        def moba_S(h, c):
            dg = c - 4 * j
            q0 = 0 if dg <= 0 else dg * 128
            b = ps_rot()
            k.op("pe", lambda: pe.matmul(PS[b][:, q0:512], lhsT=T_KT[0:80, h, c * 128:(c + 1) * 128], rhs=QT(h)[:, q0:512],
                                         start=True, stop=True), [bKT[h], bKT1[h], bQT[h]], [bPS[b]])
            if dg >= 0:
                k.op("pe", lambda: pe.matmul(PS[b][:, dg * 128:(dg + 1) * 128], lhsT=ident[:], rhs=tri[:], start=False, stop=True,
                                             skip_group_check=True), [bC, bC2], [bPS[b]])
            r = pti[0] % 3
            pti[0] += 1
            k.op("act", lambda: act.activation(out=PTv(r)[:, q0:512], in_=PS[b][:, q0:512], func=AF.Exp, scale=0.125), [bPS[b]], [bPT[r]])
            return (r, q0)

        def moba_PV(h, c, rq):
            r, q0 = rq
            if c == 0:
                a = ps_acc()
                hstate[h] = a
                k.op("pe", lambda: pe.matmul(PS[a][:, 0:260], lhsT=zer[:, 0:128], rhs=zer[:, 0:260], start=True, stop=True), [bC2], [bPS[a]])
            a = hstate[h]
            av = PS[a][:, 0:260].rearrange("p (a b) -> p a b", a=4)
            for st in range(q0 // 128, NST):
                k.op("pe", lambda st=st: pe.matmul(av[:, st, 0:65], lhsT=PTv(r)[:, st * 128:(st + 1) * 128], rhs=T_V[:, c, h, 0:65],
                                                   start=False, stop=True, skip_group_check=True), [bPT[r], bV], [bPS[a]])
            if c == nchunk - 1:
                ri = h % 2
                k.op("dve", lambda: dve.reciprocal(out=rcS[ri][:, 0:4], in_=av[:, :, 64]), [bPS[a]], [brc[ri]])
                k.op("dve", lambda: dve.tensor_tensor(out=T_X[:, :, h * 64:(h + 1) * 64], in0=av[:, :, 0:64],
                                                      in1=rcS[ri][:, 0:4].unsqueeze(2).to_broadcast([128, 4, 64]), op=ALU.mult),
                     [bPS[a], brc[ri]], bX)

        items = []
        for h in range(8):
            for c in range(nchunk):
                items.append((lambda h=h, c=c: moba_S(h, c), lambda rq, h=h, c=c: moba_PV(h, c, rq)))
        pipeline(items)
        for which in (0, 1):
            for cp in range(2):
                b = ps_rot()
                for c in (2 * cp, 2 * cp + 1):
                    for st in range(NST):
                        k.op("pe", lambda c=c, st=st, b=b: pe.transpose(
                            out=psb16(b)[:, (c % 2) * 512 + st * 128:(c % 2) * 512 + (st + 1) * 128],
                            in_=T_X[:, st, which * 512 + c * 128: which * 512 + (c + 1) * 128], identity=ident[:]), [bX[st], bC], [bPS[b]])
                for c in (2 * cp, 2 * cp + 1):
                    src = psb16(b)[:, (c % 2) * 512:(c % 2) * 512 + 512]
                    if which == 0:
                        evac_copy(yTm(c), src, [bPS[b]], [bY2[c // 2]])
                    else:
                        evac_copy(yTe(c), src, [bPS[b]], [bY2[2 + c // 2]])
        for st in range(NST):
            if seq == 0 and j == 0:
                k.dma("sp", f"xld{st}", lambda st=st: nc.sync.dma_start(out=xf(st), in_=x_d[seq, j * T + st * 128:j * T + (st + 1) * 128, :]), [], [bX[st]])
            else:
                k.dma("pool", f"xld{st}", lambda st=st: nc.gpsimd.dma_start(out=xf(st), in_=x_d[seq, j * T + st * 128:j * T + (st + 1) * 128, :]), [], [bX[st]])
        ysrc = ((yTc, [bYC]), (yTm, [bY2[0], bY2[1]]), (yTe, [bY2[2], bY2[3]]))
        for i in range(3):
            yfn, ybufs = ysrc[i]
            for dp in range(4):
                sg_, sb_ = w_in_grp(14 + 4 * i + dp), w_br_grp(i, dp)
                for c in (2 * dp, 2 * dp + 1):
                    off = (c % 2) * 128
                    bg, bp = ps_rot(), ps_rot()
                    for kc in range(8):
                        k.op("pe", lambda kc=kc: pe.matmul(PS[bg][:], lhsT=sg_[0][:, kc, off:off + 128], rhs=T_hT[:, kc, :], start=(kc == 0), stop=(kc == 7)),
                             bhT + [sg_[1]], [bPS[bg]])
                    for kc in range(4):
                        k.op("pe", lambda kc=kc: pe.matmul(PS[bp][:], lhsT=sb_[0][:, kc, off:off + 128], rhs=yfn(kc), start=(kc == 0), stop=(kc == 3)),
                             ybufs + [sb_[1]], [bPS[bp]])
                    ti = c % 2
                    k.op("act", lambda: act.activation(out=tmpf(ti), in_=PS[bg][:], func=AF.Sigmoid,
                                                       bias=colsA[:, CA["bgate"] + i * 8 + c:CA["bgate"] + i * 8 + c + 1], scale=1.0),
                         [bPS[bg], bC], [bTMP[ti]])
                    if i == 0:
                        k.op("dve", lambda: dve.tensor_tensor(out=macc(c), in0=PS[bp][:], in1=tmpf(ti), op=ALU.mult), [bPS[bp], bTMP[ti]], [bACC])
                    else:
                        k.op("dve", lambda: dve.tensor_tensor(out=tmpf(2 + ti), in0=PS[bp][:], in1=tmpf(ti), op=ALU.mult),
                             [bPS[bp], bTMP[ti]], [bTMP[2 + ti]])
                        k.op("pool", lambda: pool.tensor_tensor(out=macc(c), in0=macc(c), in1=tmpf(2 + ti), op=ALU.add), [bACC, bTMP[2 + ti]], [bACC])
        wos = [w_o_grp(g) for g in range(4)]
        for st in range(NST):
            for g in range(4):
                wv, wb = wos[g]
                b = ps_rot()
                for kc in range(8):
                    k.op("pe", lambda kc=kc: pe.matmul(PS[b][:, 0:256], lhsT=macc(kc)[:, st * 128:(st + 1) * 128], rhs=wv[:, kc, :],
                                                       start=(kc == 0), stop=(kc == 7)), [bACC, wb], [bPS[b]])
                k.op("dve", lambda: dve.tensor_tensor(out=xf(st)[:, g * 256:(g + 1) * 256], in0=PS[b][:, 0:256],
                                                      in1=xf(st)[:, g * 256:(g + 1) * 256], op=ALU.add), [bPS[b], bX[st]], [bX[st]])
            normA(st, [bX[st]])
            if st >= 1:
                normB(st - 1, CA["gffn"])
        normB(NST - 1, CA["gffn"])
        k.fence(ffn_bufs)
        fcw, fcb = 0, CA["fcb"]
        for gi in range(11):
            sa, sb_ = w_up_grp(gi), w_up_grp(11 + gi)
            for m in (2 * gi, 2 * gi + 1):
                off = (m % 2) * 128
                ba, bb = ps_rot(), ps_rot()
                for kc in range(8):
                    k.op("pe", lambda kc=kc: pe.matmul(PS[ba][:], lhsT=sa[0][:, kc, off:off + 128], rhs=T_hT[:, kc, :], start=(kc == 0), stop=(kc == 7)),
                         bhT + [sa[1]], [bPS[ba]])
                for kc in range(8):
                    k.op("pe", lambda kc=kc: pe.matmul(PS[bb][:], lhsT=sb_[0][:, kc, off:off + 128], rhs=T_hT[:, kc, :], start=(kc == 0), stop=(kc == 7)),
                         bhT + [sb_[1]], [bPS[bb]])
                a_ = asb(m % 2)
                ab_ = bASB[m % 2]
                k.op("act", lambda: act.copy(out=a_[:, 2:514], in_=PS[ba][:]), [bPS[ba]], [ab_])
                k.op("pool", lambda: pool.tensor_copy(out=a_[:, 0:2], in_=ah[:, m, :]), [bah], [ab_])
                k.op("pool", lambda: pool.tensor_copy(out=ah[:, m, :], in_=a_[:, 512:514]), [ab_], [bah])
                k.op("act", lambda: act.activation(out=tmpf(0), in_=PS[ba][:], func=AF.Identity, bias=colsA[:, fcb + m:fcb + m + 1],
                                                   scale=colsB[:, fcw + 44 + m:fcw + 44 + m + 1]), [bPS[ba], bC], [bTMP[0]])
                k.op("dve", lambda: dve.scalar_tensor_tensor(out=tmpf(1), in0=a_[:, 1:513], scalar=colsB[:, fcw + 22 + m:fcw + 22 + m + 1], in1=tmpf(0),
                                                             op0=ALU.mult, op1=ALU.add), [ab_, bC, bTMP[0]], [bTMP[1]])
                k.op("dve", lambda: dve.scalar_tensor_tensor(out=tmpf(2), in0=a_[:, 0:512], scalar=colsB[:, fcw + m:fcw + m + 1], in1=tmpf(1),
                                                             op0=ALU.mult, op1=ALU.add), [ab_, bC, bTMP[1]], [bTMP[2]])
                k.op("act", lambda: act.activation(out=tmpf(3), in_=tmpf(2), func=AF.Silu), [bTMP[2]], [bTMP[3]])
                k.op("dve", lambda: dve.tensor_tensor(out=actT(m), in0=PS[bb][:], in1=tmpf(3), op=ALU.mult), [bPS[bb], bTMP[3]], [bACT])
        for cg in range(4):
            banks = [ps_rot() for _ in range(NST)]
            for (m0, cnt) in ((0, 8), (8, 8), (16, 6)):
                wv, wb = w_dn_grp(cg, m0, cnt)
                for mi in range(cnt):
                    m = m0 + mi
                    for st in range(NST):
                        b = banks[st]
                        k.op("pe", lambda m=m, mi=mi, st=st, b=b: pe.matmul(PS[b][:, 0:256], lhsT=actT(m)[:, st * 128:(st + 1) * 128], rhs=wv[:, mi, :],
                                                                            start=(m == 0), stop=(m == NCH_FF - 1)), [bACT, wb], [bPS[b]])
            for st in range(NST):
                b = banks[st]
                k.op("dve", lambda b=b, st=st: dve.tensor_tensor(out=xf(st)[:, cg * 256:(cg + 1) * 256], in0=PS[b][:, 0:256],
                                                                 in1=xf(st)[:, cg * 256:(cg + 1) * 256], op=ALU.add), [bPS[b], bX[st]], [bX[st]])
            if do_early:
                if cg == 0:
                    early_A(seq, j + 1, 0)
                    early_A(seq, j + 1, 1)
                elif cg == 1:
                    early_B(0)
                    early_B(1)
                    early_A(seq, j + 1, 2)
                    early_A(seq, j + 1, 3)
                elif cg == 2:
                    early_B(2)
                    early_B(3)
        for st in range(NST):
            k.dma("pool", f"ost{st}", lambda st=st: nc.gpsimd.dma_start(out=out_d[seq, j * T + st * 128:j * T + (st + 1) * 128, :], in_=xf(st)), [bX[st]], [])

    for seq in range(NSEQ):
        if seq > 0:
            k.fence(mixer_bufs)
        mem_prologue(seq)
        for j in range(NT):
            tile(seq, j, p0_done=(j > 0), do_early=(j < NT - 1))
    k.final_wait("sp")
    k.final_wait("pool")
    es.close()
    return nc


_NAMES = ["x", "mem", "g_mix", "w_in", "b_gate", "conv_w", "conv_b", "moba_q_gain", "moba_k_gain", "g_mem", "w_mem_kv",
          "memq_gain", "memk_gain", "w_br_conv", "w_br_moba", "w_br_mem", "w_o", "g_ffn", "w_up", "ffn_conv_w",
          "ffn_conv_b", "w_down"]


def kernel(**inputs):
    ncores = 8
    x = np.ascontiguousarray(np.asarray(inputs["x"], dtype=np.float32))
    mem = np.ascontiguousarray(np.asarray(inputs["mem"], dtype=np.float32))
    bsz, s, _ = x.shape
    per = bsz // ncores
    nc = build_nc(per, s // T)
    shared = {n: np.ascontiguousarray(np.asarray(inputs[n], dtype=np.float32)) for n in _NAMES[2:]}
    in_maps = []
    for c in range(ncores):
        m = dict(shared)
        m["x"] = x[c * per:(c + 1) * per]
        m["mem"] = mem[c * per:(c + 1) * per]
        in_maps.append(m)
    res = run_bass_kernel_spmd(nc, in_maps, core_ids=list(range(ncores)))
    return np.concatenate([np.asarray(r["out"]) for r in res.results], axis=0).astype(np.float32)
```

```python
import numpy as np
from contextlib import ExitStack
import concourse.bass as bass
import concourse.mybir as mybir
from concourse.bass_utils import run_bass_kernel_spmd

F32 = mybir.dt.float32
BF16 = mybir.dt.bfloat16
AF = mybir.ActivationFunctionType
ALU = mybir.AluOpType
AX = mybir.AxisListType

D = 1024
T = 512
NST = 4
IN_COLS = 6656
D_FF = 2816
NCH_FF = 22
MEM_LEN = 256
EPS = 1e-6
NSLOT = 5
NEG = -30000.0
SYNC_SAME = True


class Eng:
    def __init__(self, name, handle, sem, step=1, inorder=False):
        self.name, self.h, self.sem, self.step, self.inorder = name, handle, sem, step, inorder
        self.cnt = 0
        self.waited = {}


class Buf:
    __slots__ = ("name", "w", "r")

    def __init__(self, name):
        self.name, self.w, self.r = name, None, {}


class K:
    def __init__(self, nc, es):
        self.nc = nc
        self.es = es
        self.engs = {}
        for nm, h, ino in (("pe", nc.tensor, True), ("act", nc.scalar, False), ("dve", nc.vector, False),
                           ("pool", nc.gpsimd, False), ("sp", nc.sync, False)):
            self.engs[nm] = Eng(nm, h, es.enter_context(nc.semaphore("s_" + nm)), 1, ino)
        self.chans = {}

    def chan(self, name):
        if name not in self.chans:
            self.chans[name] = Eng(name, None, self.es.enter_context(self.nc.semaphore("c_" + name)), 16)
        return self.chans[name]

    def _waits(self, eng, reads, writes):
        deps = {}
        relax = eng.name in ("act", "dve")

        def add(e, c):
            if deps.get(e, 0) < c:
                deps[e] = c
        for b in reads:
            if b.w is not None:
                add(*b.w)
        for b in writes:
            if b.w is not None and not (relax and b.w[0] is eng):
                add(*b.w)
            for e, c in b.r.items():
                if relax and e is eng:
                    continue
                add(e, c)
        for e, c in deps.items():
            if e is eng and (eng.inorder or not SYNC_SAME):
                continue
            if eng.waited.get(e, 0) < c:
                eng.h.wait_ge(e.sem, c)
                eng.waited[e] = c

    def _mark(self, me_e, me_c, reads, writes):
        for b in reads:
            if b.r.get(me_e, 0) < me_c:
                b.r[me_e] = me_c
        for b in writes:
            b.w = (me_e, me_c)
            b.r = {}

    def op(self, en, fn, reads=(), writes=()):
        eng = self.engs[en]
        self._waits(eng, reads, writes)
        ins = fn()
        eng.cnt += 1
        ins.then_inc(eng.sem, 1)
        self._mark(eng, eng.cnt, reads, writes)

    def dma(self, qn, chname, fn, reads=(), writes=()):
        q = self.engs[qn]
        ch = self.chan(chname)
        self._waits(q, reads, writes)
        ins = fn()
        ch.cnt += 16
        ins.then_inc(ch.sem, 16)
        self._mark(ch, ch.cnt, reads, writes)

    def fence(self, bufs):
        snap = {}
        for e in list(self.engs.values()) + list(self.chans.values()):
            if e.cnt > 0:
                snap[e] = e.cnt
        for b in bufs:
            for e, c in snap.items():
                if b.r.get(e, 0) < c:
                    b.r[e] = c

    def final_wait(self, qn):
        q = self.engs[qn]
        for e in list(self.engs.values()) + list(self.chans.values()):
            if e is q or e.cnt == 0:
                continue
            if q.waited.get(e, 0) < e.cnt:
                q.h.wait_ge(e.sem, e.cnt)
                q.waited[e] = e.cnt


def build_nc(NSEQ, NT):
    S = NT * T
    NBLK = 16
    NCHK = NT * 4
    nc = bass.Bass("TRN2", target_bir_lowering=False)
    es = ExitStack()
    k = K(nc, es)
    es.enter_context(nc.allow_low_precision("bf16 matmul operands by design; reductions accumulate in fp32 internally"))

    def din(name, shape):
        return nc.dram_tensor(name, list(shape), F32, kind="ExternalInput").ap()

    x_d = din("x", (NSEQ, S, D))
    mem_d = din("mem", (NSEQ, MEM_LEN, D))
    g_mix_d = din("g_mix", (D,))
    w_in_d = din("w_in", (D, IN_COLS))
    b_gate_d = din("b_gate", (3 * D,))
    conv_w_d = din("conv_w", (3, 512))
    conv_b_d = din("conv_b", (512,))
    qg_d = din("moba_q_gain", (64,))
    kg_d = din("moba_k_gain", (64,))
    g_mem_d = din("g_mem", (D,))
    w_mkv_d = din("w_mem_kv", (D, 1024))
    mqg_d = din("memq_gain", (128,))
    mkg_d = din("memk_gain", (128,))
    w_brc_d = din("w_br_conv", (512, D))
    w_brm_d = din("w_br_moba", (512, D))
    w_bre_d = din("w_br_mem", (512, D))
    w_o_d = din("w_o", (D, D))
    g_ffn_d = din("g_ffn", (D,))
    w_up_d = din("w_up", (D, 2 * D_FF))
    fcw_d = din("ffn_conv_w", (3, D_FF))
    fcb_d = din("ffn_conv_b", (D_FF,))
    w_dn_d = din("w_down", (D_FF, D))
    out_d = nc.dram_tensor("out", [NSEQ, S, D], F32, kind="ExternalOutput").ap()

    def dscr(name, shape):
        return nc.dram_tensor(name, list(shape), BF16, kind="Internal").ap()

    w_in_b = dscr("w_in_b", (D, IN_COLS))
    w_mkv_b = dscr("w_mkv_b", (D, 1024))
    w_br_b = [dscr("w_brc_b", (512, D)), dscr("w_brm_b", (512, D)), dscr("w_bre_b", (512, D))]
    w_o_b = dscr("w_o_b", (D, D))
    w_up_b = dscr("w_up_b", (D, 2 * D_FF))
    w_dn_b = dscr("w_dn_b", (D_FF, D))

    def sb(name, shape, dt):
        return es.enter_context(nc.sbuf_tensor(name, list(shape), dt))

    T_X = sb("T_X", [128, NST, 2048], BF16)
    T_hT = sb("T_hT", [128, 8, T], BF16)
    T_KT = sb("T_KT", [128, 8, S], BF16)
    T_V = sb("T_V", [128, NCHK, 8, 66], BF16)
    T_W = sb("T_W", [128, NSLOT, 2048], BF16)
    T_M = sb("T_M", [128, 23048], BF16)
    ident = sb("ident", [128, 128], BF16)
    identf = sb("identf", [128, 128], F32)
    tri = sb("tri", [128, 128], BF16)
    zer = sb("zer", [128, 260], BF16)
    pen = sb("pen", [128, 16, 16], F32)
    colsA = sb("colsA", [128, 128], F32)
    colsB = sb("colsB", [128, 128], F32)
    g2 = sb("g2", [128, 64], F32)
    g2k = sb("g2k", [128, 64], F32)
    mg2 = sb("mg2", [128, 128], F32)
    mg2k = sb("mg2k", [128, 128], F32)
    eps_c = sb("eps_c", [128, 1], F32)
    KmT = sb("KmT", [128, 4, MEM_LEN], BF16)
    Vm = sb("Vm", [128, 2, 4, 130], BF16)
    kbarT = sb("kbarT", [128, 8, 16], BF16)
    uh = sb("uh", [128, 4, 2], F32)
    ah = sb("ah", [128, NCH_FF, 2], F32)
    st_ss = sb("st_ss", [128, 8], F32)
    st_rms = sb("st_rms", [128, 8], F32)
    st_rstd = sb("st_rstd", [128, 8], F32)
    n_ss = [sb(f"n_ss{i}", [128, 8], F32) for i in range(2)]
    n_rms = [sb(f"n_rms{i}", [128, 8], F32) for i in range(2)]
    n_rstd = [sb(f"n_rstd{i}", [128, 8], F32) for i in range(2)]
    gmS = [sb(f"gm{i}", [128, 8, 16], F32) for i in range(1)]
    mxS = [sb(f"mx{i}", [128, 8, 8], F32) for i in range(1)]
    thrS = [sb(f"thr{i}", [128, 8], F32) for i in range(1)]
    selS = [sb(f"sel{i}", [128, 8, 16], F32) for i in range(1)]
    nmS = [sb(f"nm{i}", [128, 8, 64], BF16) for i in range(4)]
    rcS = [sb(f"rc{i}", [128, 4], F32) for i in range(2)]
    PS = [es.enter_context(nc.psum_tensor(f"ps{i}", [128, 512], F32)) for i in range(8)]

    B = {}

    def bf(name):
        if name not in B:
            B[name] = Buf(name)
        return B[name]

    bX = [bf(f"X{i}") for i in range(NST)]
    bhT = [bf(f"hT{i}") for i in range(NST)]
    bKT = [bf(f"KT{h}") for h in range(8)]
    bKT1 = [bf(f"KT1_{h}") for h in range(8)]
    bV = bf("V")
    bW = [bf(f"W{i}") for i in range(NSLOT)]
    bPS = [bf(f"PS{i}") for i in range(8)]
    bC = bf("consts")
    bC2 = bf("consts2")
    bRows, bId, bG = bf("rows"), bf("idc"), bf("gains")
    bKm, bVm, bkbar, buh, bah = bf("KmT"), bf("Vm"), bf("kbar"), bf("uh"), bf("ah")
    bstat = [bf(f"stat{i}") for i in range(NST)]
    bn = [bf("nstat0"), bf("nstat1")]
    bgate = [bf("gate0"), bf("gate1")]
    bnm = [bf(f"nm{i}") for i in range(4)]
    brc = [bf("rc0"), bf("rc1")]
    O_QT, O_QM, O_PT, O_Y2, O_YC, O_TMP, O_ACC, O_YU = 0, 4096, 6144, 7680, 11776, 13824, 17920, 22016
    bU = bf("u")
    bQT = [bf(f"QT{h}") for h in range(8)]
    bQm = bf("QmT")
    bPT = [bf(f"PT{i}") for i in range(3)]
    bY2 = [bf(f"Y2_{i}") for i in range(4)]
    bYC = bf("yTc")
    bTMP = [bf(f"tmp{i}") for i in range(4)]
    bACC = bf("macc")
    bACT = bf("actT")
    bASB = [bf("asb0"), bf("asb1")]
    mixer_bufs = bQT + [bQm] + bPT + bY2 + [bYC] + bTMP + [bACC]
    ffn_bufs = [bACT] + bASB + bTMP

    def QT(h):
        return T_M[0:80, O_QT + h * T: O_QT + (h + 1) * T]

    def QmTv(h):
        return T_M[:, O_QM + h * T: O_QM + (h + 1) * T]

    def PTv(i):
        return T_M[:, O_PT + i * T: O_PT + (i + 1) * T]

    def hpre(st):
        return T_M[:, O_Y2 + st * 1024: O_Y2 + (st + 1) * 1024]

    def yTm(c):
        return T_M[:, O_Y2 + c * T: O_Y2 + (c + 1) * T]

    def yTe(c):
        return T_M[:, O_Y2 + 2048 + c * T: O_Y2 + 2048 + (c + 1) * T]

    def yTc(c):
        return T_M[:, O_YC + c * T: O_YC + (c + 1) * T]

    def tmpf(i, n=512, off=0):
        return T_M[:, O_TMP + i * 1024: O_TMP + (i + 1) * 1024].bitcast(F32)[:, off:off + n]

    def tmpb(i):
        return T_M[:, O_TMP + i * 1024: O_TMP + (i + 1) * 1024]

    def macc(c):
        return T_M[:, O_ACC + c * T: O_ACC + (c + 1) * T]

    def actT(m):
        return T_M[:, m * T:(m + 1) * T]

    def asb(i):
        o = 11264 + i * 1028
        return T_M[:, o:o + 1028].bitcast(F32)

    def xf(st):
        return T_X[:, st, :].bitcast(F32)

    def qkpre(st):
        return T_X[:, st, 0:1024]

    def qmpre(st):
        return T_X[:, st, 1024:1536]

    rot = [0]

    def ps_rot():
        b = rot[0] % 5
        rot[0] += 1
        return b

    accb = [0]

    def ps_acc():
        b = 5 + (accb[0] % 2)
        accb[0] += 1
        return b

    def psb16(b):
        return PS[b][:].bitcast(BF16)

    altc = [0]

    def evac_copy(out, in_, reads, writes):
        altc[0] += 1
        if altc[0] % 2 == 0:
            k.op("act", lambda: nc.scalar.copy(out=out, in_=in_), reads, writes)
        else:
            k.op("dve", lambda: nc.vector.tensor_copy(out=out, in_=in_), reads, writes)

    wslot = [0]

    def wload(src_ap, nkc, srcbuf):
        s = wslot[0] % NSLOT
        wslot[0] += 1
        dst = T_W[:, s, 0:nkc * 256].rearrange("p (a b) -> p a b", a=nkc)
        k.dma("sp", f"w{s}", lambda: nc.sync.dma_start(out=dst, in_=src_ap), [srcbuf], [bW[s]])
        return dst, bW[s]

    bsrc = {n: bf("src_" + n) for n in ("winA", "winB", "winC", "mkv", "br0", "br1", "br2", "wo", "wup", "wdn")}

    def w_in_src(gi):
        c = gi * 256
        return bsrc["winA"] if 1536 <= c < 3584 else (bsrc["winB"] if c < 1536 else bsrc["winC"])

    def w_in_grp(gi):
        return wload(w_in_b[:, gi * 256:(gi + 1) * 256].rearrange("(kc p) n -> p kc n", p=128), 8, w_in_src(gi))

    def w_mkv_grp(gi):
        return wload(w_mkv_b[:, gi * 256:(gi + 1) * 256].rearrange("(kc p) n -> p kc n", p=128), 8, bsrc["mkv"])

    def w_br_grp(i, gi):
        return wload(w_br_b[i][:, gi * 256:(gi + 1) * 256].rearrange("(kc p) n -> p kc n", p=128), 4, bsrc[f"br{i}"])

    def w_o_grp(gi):
        return wload(w_o_b[:, gi * 256:(gi + 1) * 256].rearrange("(kc p) n -> p kc n", p=128), 8, bsrc["wo"])

    def w_up_grp(gi):
        return wload(w_up_b[:, gi * 256:(gi + 1) * 256].rearrange("(kc p) n -> p kc n", p=128), 8, bsrc["wup"])

    def w_dn_grp(cg, m0, cnt):
        return wload(w_dn_b[m0 * 128:(m0 + cnt) * 128, cg * 256:(cg + 1) * 256].rearrange("(m p) n -> p m n", p=128),
                     cnt, bsrc["wdn"])

    def cast(dst, src, rows, name, nsplit, after=()):
        step = rows // nsplit
        for i in range(nsplit):
            k.dma("pool", "cast_" + name,
                  lambda i=i: nc.gpsimd.dma_start(out=dst[i * step:(i + 1) * step, :], in_=src[i * step:(i + 1) * step, :]),
                  list(after), [bsrc[name]])

    def cast_win():
        for name, c0, c1 in (("winA", 1536, 3584), ("winB", 0, 1536), ("winC", 3584, IN_COLS)):
            for i in range(4):
                k.dma("pool", "cast_" + name,
                      lambda i=i, c0=c0, c1=c1: nc.gpsimd.dma_start(out=w_in_b[i * 256:(i + 1) * 256, c0:c1], in_=w_in_d[i * 256:(i + 1) * 256, c0:c1]),
                      [], [bsrc[name]])

    bsrc["wo"].w = bsrc["wup"].w = bsrc["wdn"].w = None
    deferred = [True]

    def deferred_casts():
        if not deferred[0]:
            return
        deferred[0] = False
        cast(w_o_b, w_o_d, D, "wo", 2, after=[bsrc["winC"]])
        cast(w_up_b, w_up_d, D, "wup", 8, after=[bsrc["winC"]])
        cast(w_dn_b, w_dn_d, D_FF, "wdn", 4, after=[bsrc["winC"]])

    pool, dve, act, pe = nc.gpsimd, nc.vector, nc.scalar, nc.tensor
    for st in range(2):
        k.dma("pool", f"xld{st}", lambda st=st: nc.gpsimd.dma_start(out=T_X[:, st, :].bitcast(F32), in_=mem_d[0, st * 128:(st + 1) * 128, :]), [], [bX[st]])
    rowsA = T_M[:, O_TMP:O_TMP + 256].bitcast(F32)
    rowsB = T_M[:, O_TMP + 1024:O_TMP + 1280].bitcast(F32)
    k.op("pool", lambda: pool.memset(rowsA, 0.0), [], [bRows, bTMP[0]])
    k.op("pool", lambda: pool.memset(rowsB, 0.0), [], [bRows, bTMP[1]])

    def rowload(dst_t, r0, src, n):
        k.dma("sp", "cst", lambda: nc.sync.dma_start(out=dst_t[r0:r0 + n, :], in_=src.rearrange("(r c) -> r c", c=128)), [], [bRows, bTMP[0], bTMP[1]])

    rowload(rowsA, 0, g_mix_d, 8)
    rowload(rowsA, 8, g_ffn_d, 8)
    rowload(rowsA, 16, g_mem_d, 8)
    rowload(rowsA, 24, b_gate_d, 24)
    rowload(rowsA, 48, conv_w_d.rearrange("a b -> (a b)"), 12)
    rowload(rowsA, 60, conv_b_d, 4)
    rowload(rowsA, 64, fcb_d, 22)
    rowload(rowsB, 0, fcw_d.rearrange("a b -> (a b)"), 66)
    CA = dict(gmix=0, gffn=8, gmem=16, bgate=24, convw=48, convb=60, fcb=64)
    for dstt, src, n in ((g2, qg_d, 64), (g2k, kg_d, 64), (mg2, mqg_d, 128), (mg2k, mkg_d, 128)):
        k.dma("sp", "cst", lambda dstt=dstt, src=src, n=n: nc.sync.dma_start(out=dstt[:, 0:n], in_=src.partition_broadcast(128)), [], [bG])
    k.op("pool", lambda: pool.memset(eps_c[:], EPS), [], [bId])
    k.op("pool", lambda: pool.memset(ident[:], 1.0), [], [bId])
    k.op("pool", lambda: pool.affine_select(out=ident[:], in_=ident[:], pattern=[[-1, 128]], compare_op=ALU.is_equal,
                                             fill=0.0, base=0, channel_multiplier=1), [bId], [bId])
    k.op("pool", lambda: pool.memset(identf[:], 1.0), [], [bId])
    k.op("pool", lambda: pool.affine_select(out=identf[:], in_=identf[:], pattern=[[-1, 128]], compare_op=ALU.is_equal,
                                             fill=0.0, base=0, channel_multiplier=1), [bId], [bId])
    k.op("pool", lambda: pool.memset(Vm[:, :, :, 128:130], 1.0), [], [bVm])
    k.op("pool", lambda: pool.memset(zer[:], 0.0), [], [bC2])
    k.op("pool", lambda: pool.memset(kbarT[:], 0.0), [], [bkbar])
    k.op("pool", lambda: pool.memset(T_V[:, :, :, 64:66], 1.0), [], [bV])
    for i in range(4):
        k.op("pool", lambda i=i: pool.memset(nmS[i][:], 0.0), [], [bnm[i]])
    cast(w_mkv_b, w_mkv_d, D, "mkv", 2)
    cast_win()
    for i, wd in enumerate((w_brc_d, w_brm_d, w_bre_d)):
        cast(w_br_b[i], wd, 512, f"br{i}", 1)
    for rows_t, cols_t, nr in ((rowsA, colsA, 86), (rowsB, colsB, 66)):
        b = ps_rot()
        k.op("pe", lambda rows_t=rows_t, b=b, nr=nr: pe.transpose(out=PS[b][:, 0:nr], in_=rows_t[0:nr, :], identity=identf[0:nr, 0:nr]),
             [bRows, bId, bTMP[0], bTMP[1]], [bPS[b]])
        k.op("dve", lambda cols_t=cols_t, b=b, nr=nr: dve.tensor_copy(out=cols_t[:, 0:nr], in_=PS[b][:, 0:nr]), [bPS[b]], [bC])
    k.op("dve", lambda: dve.tensor_tensor(out=g2[:], in0=g2[:], in1=g2k[:], op=ALU.mult), [bG], [bG])
    k.op("dve", lambda: dve.tensor_tensor(out=mg2[:], in0=mg2[:], in1=mg2k[:], op=ALU.mult), [bG], [bG])
    k.op("pool", lambda: pool.affine_select(out=tri[:], in_=zer[:, 0:128], pattern=[[1, 128]], compare_op=ALU.is_ge,
                                             fill=NEG, base=0, channel_multiplier=-1), [bC2], [bC2])
    k.op("pool", lambda: pool.memset(pen[:], 0.0), [], [bC2])
    k.op("pool", lambda: pool.affine_select(out=pen[:], in_=pen[:], pattern=[[1, 16], [-1, 16]], compare_op=ALU.is_ge,
                                             fill=-1e30, base=-1, channel_multiplier=0), [bC2], [bC2])
    k.op("pool", lambda: pool.memset(T_KT[64:80, 0, :], 1.0), [], [bKT1[0]])
    k.op("pool", lambda: pool.affine_select(out=T_KT[64:80, 0, :], in_=T_KT[64:80, 0, :], pattern=[[1, S]],
                                             compare_op=ALU.is_ge, fill=0.0, base=0, channel_multiplier=-256), [], [bKT1[0]])
    k.op("pool", lambda: pool.affine_select(out=T_KT[64:80, 0, :], in_=T_KT[64:80, 0, :], pattern=[[-1, S]],
                                             compare_op=ALU.is_ge, fill=0.0, base=255, channel_multiplier=256), [], [bKT1[0]])
    late = [True]

    def late_consts():
        if not late[0]:
            return
        late[0] = False
        for h in range(1, 8):
            k.op("dve", lambda h=h: dve.tensor_copy(out=T_KT[64:80, h, :], in_=T_KT[64:80, 0, :]), [bKT1[0]], [bKT1[h]])

    def normA(st, xbuf, src=None, junk=None, junkb=None, hp=None, hpb=None):
        src = xf(st) if src is None else src
        junk = tmpb(2 + st % 2) if junk is None else junk
        junkb = [bTMP[2 + st % 2]] if junkb is None else junkb
        hp = hpre(st) if hp is None else hp
        hpb = [bY2[st]] if hpb is None else hpb
        k.op("act", lambda: act.activation(out=junk, in_=src, func=AF.Square, accum_out=st_ss[:, st:st + 1]),
             list(xbuf), junkb + [bstat[st]])
        k.op("act", lambda: act.activation(out=st_rms[:, st:st + 1], in_=st_ss[:, st:st + 1], func=AF.Sqrt, bias=eps_c[:], scale=1.0 / D),
             [bstat[st], bC], [bstat[st]])
        k.op("dve", lambda: dve.reciprocal(out=st_rstd[:, st:st + 1], in_=st_rms[:, st:st + 1]), [bstat[st]], [bstat[st]])
        k.op("act", lambda: act.activation(out=hp, in_=src, func=AF.Identity, scale=st_rstd[:, st:st + 1]),
             list(xbuf) + [bstat[st]], hpb)

    def normB(st, gcol0, hp=None, hpb=None, bank=None):
        hp = hpre(st) if hp is None else hp
        hpb = [bY2[st]] if hpb is None else hpb
        b = ps_rot() if bank is None else bank
        for kc in range(8):
            k.op("pe", lambda kc=kc: pe.transpose(out=psb16(b)[:, kc * 128:(kc + 1) * 128], in_=hp[:, kc * 128:(kc + 1) * 128],
                                                  identity=ident[:]), hpb + [bC], [bPS[b]])
        k.op("dve", lambda: dve.tensor_tensor(out=T_hT[:, :, st * 128:(st + 1) * 128], in0=psb16(b).rearrange("p (a b) -> p a b", a=8),
                                              in1=colsA[:, gcol0:gcol0 + 8].unsqueeze(2).to_broadcast([128, 8, 128]), op=ALU.mult),
             [bPS[b], bC], [bhT[st]])

    def norm_to_hT(nst, gcol0, xbufs):
        for st in range(nst):
            normA(st, [xbufs[st]])
            if st >= 1:
                normB(st - 1, gcol0)
        normB(nst - 1, gcol0)

    def xs(i):
        return T_M[:, O_TMP + i * 2048:O_TMP + (i + 1) * 2048].bitcast(F32)

    def early_A(seq, j1, st):
        i = st % 2
        xb = [bTMP[2 * i], bTMP[2 * i + 1]]
        k.dma("pool", f"xs{i}", lambda: nc.gpsimd.dma_start(out=xs(i), in_=x_d[seq, j1 * T + st * 128:j1 * T + (st + 1) * 128, :]), [], xb)
        normA(st, xb, src=xs(i), junk=T_M[:, 11264:12288], junkb=bASB, hp=T_M[:, O_ACC + st * 1024:O_ACC + (st + 1) * 1024], hpb=[bACC])

    def early_B(st):
        normB(st, CA["gmix"], hp=T_M[:, O_ACC + st * 1024:O_ACC + (st + 1) * 1024], hpb=[bACC], bank=7)

    def tok_proj(st, slots, lhs_of_kc, lhsbufs):
        b = ps_rot()
        for half, (wv, wb) in enumerate(slots):
            for kc in range(8):
                k.op("pe", lambda kc=kc, wv=wv, half=half, b=b: pe.matmul(
                    PS[b][:, half * 256:(half + 1) * 256], lhsT=lhs_of_kc(kc, st), rhs=wv[:, kc, :], start=(kc == 0), stop=(kc == 7)),
                    lhsbufs + [wb], [bPS[b]])
        return b

    def hT_lhs(kc, st):
        return T_hT[:, kc, st * 128:(st + 1) * 128]

    nsi = [0]

    def head_norm(b, nh, hd, out_ap, outbufs, gain_ap=None):
        i = nsi[0] % 2
        nsi[0] += 1
        sq = tmpf(i)
        k.op("act", lambda: act.activation(out=sq, in_=PS[b][:], func=AF.Square), [bPS[b]], [bTMP[i]])
        k.op("dve", lambda: dve.tensor_reduce(out=n_ss[i][:, 0:nh], in_=sq.rearrange("p (a b) -> p a b", a=nh), axis=AX.X, op=ALU.add),
             [bTMP[i]], [bn[i]])
        k.op("act", lambda: act.activation(out=n_rms[i][:, 0:nh], in_=n_ss[i][:, 0:nh], func=AF.Sqrt, bias=eps_c[:], scale=1.0 / hd),
             [bn[i], bC], [bn[i]])
        k.op("dve", lambda: dve.reciprocal(out=n_rstd[i][:, 0:nh], in_=n_rms[i][:, 0:nh]), [bn[i]], [bn[i]])
        rb = n_rstd[i][:, 0:nh].unsqueeze(2).to_broadcast([128, nh, hd])
        pv = PS[b][:].rearrange("p (a b) -> p a b", a=nh)
        if gain_ap is None:
            k.op("dve", lambda: dve.tensor_tensor(out=out_ap.rearrange("p (a b) -> p a b", a=nh), in0=pv, in1=rb, op=ALU.mult),
                 [bPS[b], bn[i]], outbufs)
        else:
            k.op("dve", lambda: dve.tensor_tensor(out=sq.rearrange("p (a b) -> p a b", a=nh), in0=pv, in1=rb, op=ALU.mult),
                 [bPS[b], bn[i]], [bTMP[i]])
            k.op("dve", lambda: dve.tensor_tensor(out=out_ap.rearrange("p (a b) -> p a b", a=nh), in0=sq.rearrange("p (a b) -> p a b", a=nh),
                                                  in1=gain_ap, op=ALU.mult), [bTMP[i], bG], outbufs)

    def mem_prologue(seq):
        for st in range(2):
            if seq == 0:
                break
            k.dma("pool", f"xld{st}", lambda st=st: nc.gpsimd.dma_start(out=xf(st), in_=mem_d[seq, st * 128:(st + 1) * 128, :]), [], [bX[st]])
        norm_to_hT(2, CA["gmem"], bX)
        ks = [w_mkv_grp(0), w_mkv_grp(1)]
        for st in range(2):
            b = tok_proj(st, ks, hT_lhs, [bhT[st]])
            head_norm(b, 4, 128, T_X[:, 2 + st, 0:512], [bX[2 + st]])
        vs = [w_mkv_grp(2), w_mkv_grp(3)]
        for st in range(2):
            b = tok_proj(st, vs, hT_lhs, [bhT[st]])
            k.op("act", lambda st=st, b=b: act.copy(out=Vm[:, st, :, 0:128], in_=PS[b][:].rearrange("p (a b) -> p a b", a=4)),
                 [bPS[b]], [bVm])
        for hp in range(2):
            b = ps_rot()
            for h in (2 * hp, 2 * hp + 1):
                for st in range(2):
                    k.op("pe", lambda h=h, st=st, b=b: pe.transpose(
                        out=psb16(b)[:, (h % 2) * 512 + st * 128:(h % 2) * 512 + (st + 1) * 128],
                        in_=T_X[:, 2 + st, h * 128:(h + 1) * 128], identity=ident[:]), [bX[2 + st], bC], [bPS[b]])
            for h in (2 * hp, 2 * hp + 1):
                evac_copy(KmT[:, h, :], psb16(b)[:, (h % 2) * 512:(h % 2) * 512 + 256], [bPS[b]], [bKm])
        k.op("pool", lambda: pool.memset(uh[:], 0.0), [], [buh])
        k.op("pool", lambda: pool.memset(ah[:], 0.0), [], [bah])

    def tile(seq, j, p0_done, do_early):
        k.fence(mixer_bufs)
        if not p0_done:
            for st in range(NST):
                if seq == 0:
                    k.dma("sp", f"xld{st}", lambda st=st: nc.sync.dma_start(out=xf(st), in_=x_d[seq, j * T + st * 128:j * T + (st + 1) * 128, :]), [], [bX[st]])
                else:
                    k.dma("pool", f"xld{st}", lambda st=st: nc.gpsimd.dma_start(out=xf(st), in_=x_d[seq, j * T + st * 128:j * T + (st + 1) * 128, :]), [], [bX[st]])
            norm_to_hT(NST, CA["gmix"], bX)
        for which, g0 in ((0, 6), (1, 8)):
            sl = [w_in_grp(g0), w_in_grp(g0 + 1)]
            for st in range(NST):
                b = tok_proj(st, sl, hT_lhs, [bhT[st]])
                outap = qkpre(st)[:, which * 512:(which + 1) * 512]
                if which == 0:
                    head_norm(b, 8, 64, outap, [bX[st]], gain_ap=g2[:].unsqueeze(1).to_broadcast([128, 8, 64]))
                else:
                    head_norm(b, 8, 64, outap, [bX[st]])
        sl = [w_in_grp(10), w_in_grp(11)]
        for st in range(NST):
            b = tok_proj(st, sl, hT_lhs, [bhT[st]])
            k.op("act", lambda st=st, b=b: act.copy(out=T_V[:, j * 4 + st, :, 0:64], in_=PS[b][:].rearrange("p (a b) -> p a b", a=8)),
                 [bPS[b]], [bV])
        sl = [w_in_grp(12), w_in_grp(13)]
        for st in range(NST):
            b = tok_proj(st, sl, hT_lhs, [bhT[st]])
            head_norm(b, 4, 128, qmpre(st), [bX[st]], gain_ap=mg2[:].unsqueeze(1).to_broadcast([128, 4, 128]))
        for which in (0, 1):
            for pp in range(2):
                b = ps_rot()
                for p in (2 * pp, 2 * pp + 1):
                    for st in range(NST):
                        k.op("pe", lambda p=p, st=st, b=b: pe.transpose(
                            out=psb16(b)[:, (p % 2) * 512 + st * 128:(p % 2) * 512 + (st + 1) * 128],
                            in_=qkpre(st)[:, which * 512 + p * 128: which * 512 + (p + 1) * 128], identity=ident[:]),
                            [bX[st], bC], [bPS[b]])
                for p in (2 * pp, 2 * pp + 1):
                    for e in range(2):
                        h = 2 * p + e
                        src = psb16(b)[64 * e:64 * e + 64, (p % 2) * 512:(p % 2) * 512 + 512]
                        if which == 0:
                            evac_copy(QT(h)[0:64, :], src, [bPS[b]], [bQT[h]])
                        else:
                            evac_copy(T_KT[0:64, h, j * T:(j + 1) * T], src, [bPS[b]], [bKT[h]])
                            k.op("dve", lambda h=h: dve.tensor_reduce(
                                out=kbarT[0:64, h, 2 * j:2 * j + 2], in_=T_KT[0:64, h, j * T:(j + 1) * T].rearrange("p (a b) -> p a b", a=2),
                                axis=AX.X, op=ALU.add), [bKT[h]], [bkbar])
        for hp in range(2):
            b = ps_rot()
            for h in (2 * hp, 2 * hp + 1):
                for st in range(NST):
                    k.op("pe", lambda h=h, st=st, b=b: pe.transpose(
                        out=psb16(b)[:, (h % 2) * 512 + st * 128:(h % 2) * 512 + (st + 1) * 128],
                        in_=qmpre(st)[:, h * 128:(h + 1) * 128], identity=ident[:]), [bX[st], bC], [bPS[b]])
            for h in (2 * hp, 2 * hp + 1):
                evac_copy(QmTv(h), psb16(b)[:, (h % 2) * 512:(h % 2) * 512 + 512], [bPS[b]], [bQm])
        cslots = {}

        def conv_chunk(m):
            mh = m // 2
            if m % 2 == 0:
                cslots[mh] = (w_in_grp(2 + mh), w_in_grp(4 + mh), w_in_grp(0 + mh))
            scc, scx, scb = cslots[mh]
            off = (m % 2) * 128
            banks = []
            for (wv, wb) in (scc, scx, scb):
                b = ps_rot()
                banks.append(b)
                for kc in range(8):
                    k.op("pe", lambda kc=kc, wv=wv, b=b: pe.matmul(PS[b][:], lhsT=wv[:, kc, off:off + 128], rhs=T_hT[:, kc, :],
                                                                   start=(kc == 0), stop=(kc == 7)), bhT + [wb], [bPS[b]])
            bcc, bcx, bcb = banks
            u = T_M[:, O_YU:O_YU + 1028].bitcast(F32)
            k.op("act", lambda: act.copy(out=tmpf(2), in_=PS[bcc][:]), [bPS[bcc]], [bTMP[2]])
            k.op("pool", lambda: pool.tensor_copy(out=u[:, 0:2], in_=uh[:, m, :]), [buh], [bU])
            k.op("dve", lambda: dve.tensor_tensor(out=u[:, 2:514], in0=PS[bcx][:], in1=tmpf(2), op=ALU.mult), [bPS[bcx], bTMP[2]], [bU])
            k.op("pool", lambda: pool.tensor_copy(out=uh[:, m, :], in_=u[:, 512:514]), [bU], [buh])
            cw = CA["convw"]
            k.op("act", lambda: act.activation(out=tmpf(0), in_=u[:, 2:514], func=AF.Identity, bias=colsA[:, CA["convb"] + m:CA["convb"] + m + 1],
                                               scale=colsA[:, cw + 8 + m:cw + 8 + m + 1]), [bU, bC], [bTMP[0]])
            k.op("dve", lambda: dve.scalar_tensor_tensor(out=tmpf(1), in0=u[:, 1:513], scalar=colsA[:, cw + 4 + m:cw + 4 + m + 1], in1=tmpf(0),
                                                         op0=ALU.mult, op1=ALU.add), [bU, bC, bTMP[0]], [bTMP[1]])
            k.op("dve", lambda: dve.scalar_tensor_tensor(out=tmpf(0), in0=u[:, 0:512], scalar=colsA[:, cw + m:cw + m + 1], in1=tmpf(1),
                                                         op0=ALU.mult, op1=ALU.add), [bU, bC, bTMP[1]], [bTMP[0]])
            k.op("dve", lambda: dve.tensor_tensor(out=yTc(m), in0=PS[bcb][:], in1=tmpf(0), op=ALU.mult), [bPS[bcb], bTMP[0]], [bYC])

        conv_chunk(0)
        for st in range(NST):
            for h in range(8):
                k.op("pe", lambda: pe.matmul(PS[7][:, st * 128 + h * 16:st * 128 + (h + 1) * 16], lhsT=QT(h)[0:64, st * 128:(st + 1) * 128],
                                             rhs=kbarT[0:64, h, :], start=True, stop=True), [bQT[h], bkbar], [bPS[7]])
        def gate_chain(st):
            own = 2 * j + st // 2
            gm, mx, thr, sel, nm = gmS[0], mxS[0], thrS[0], selS[0], nmS[st]
            k.op("dve", lambda: dve.tensor_tensor(out=gm[:], in0=PS[7][:, st * 128:(st + 1) * 128].rearrange("p (a b) -> p a b", a=8),
                                                  in1=pen[:, own, :].unsqueeze(1).to_broadcast([128, 8, 16]), op=ALU.add), [bPS[7], bC2], [bgate[0]])
            for h in range(8):
                k.op("dve", lambda h=h: dve.max(out=mx[:, h, :], in_=gm[:, h, :]), [bgate[0]], [bgate[0]])
            k.op("dve", lambda: dve.tensor_scalar(out=thr[:], in0=mx[:, :, 2], scalar1=-1e29, scalar2=None, op0=ALU.max), [bgate[0]], [bgate[0]])
            k.op("dve", lambda: dve.tensor_tensor(out=sel[:], in0=gm[:], in1=thr[:].unsqueeze(2).to_broadcast([128, 8, 16]), op=ALU.is_ge),
                 [bgate[0]], [bgate[0]])
            k.op("dve", lambda: dve.tensor_scalar(out=nm[:, :, 0:16], in0=sel[:], scalar1=-NEG, scalar2=NEG, op0=ALU.mult, op1=ALU.add),
                 [bgate[0]], [bnm[st]])
            k.op("dve", lambda: dve.memset(nm[:, :, own:own + 1], 0.0), [], [bnm[st]])

        pti = [0]

        def pipeline(items, LA=2):
            n = len(items)
            rs = [None] * n
            for i in range(min(LA, n)):
                rs[i] = items[i][0]()
            for i in range(n):
                if i + LA < n:
                    rs[i + LA] = items[i + LA][0]()
                items[i][1](rs[i])

        def mem_S(h, mc):
            b = ps_rot()
            k.op("pe", lambda: pe.matmul(PS[b][:], lhsT=KmT[:, h, mc * 128:(mc + 1) * 128], rhs=QmTv(h), start=True, stop=True),
                 [bKm, bQm], [bPS[b]])
            r = pti[0] % 3
            pti[0] += 1
            k.op("act", lambda: act.activation(out=PTv(r), in_=PS[b][:], func=AF.Exp, scale=128 ** -0.5), [bPS[b]], [bPT[r]])
            return r

        def mem_PV(h, mc, r):
            ab = [5, 6]
            if mc == 0:
                for a in ab:
                    k.op("pe", lambda a=a: pe.matmul(PS[a][:, 0:258], lhsT=zer[:, 0:128], rhs=zer[:, 0:258], start=True, stop=True), [bC2], [bPS[a]])
            for st in range(NST):
                a = ab[st // 2]
                k.op("pe", lambda st=st, a=a: pe.matmul(PS[a][:, (st % 2) * 129:(st % 2) * 129 + 129], lhsT=PTv(r)[:, st * 128:(st + 1) * 128],
                                                        rhs=Vm[:, mc, h, 0:129], start=False, stop=True, skip_group_check=True),
                     [bPT[r], bVm], [bPS[a]])
            if mc == 1:
                for ai, a in enumerate(ab):
                    av = PS[a][:, 0:258].rearrange("p (a b) -> p a b", a=2)
                    k.op("dve", lambda av=av, ai=ai: dve.reciprocal(out=rcS[ai][:, 0:2], in_=av[:, :, 128]), [bPS[a]], [brc[ai]])
                    k.op("dve", lambda av=av, ai=ai: dve.tensor_tensor(
                        out=T_X[:, 2 * ai:2 * ai + 2, 512 + h * 128:512 + (h + 1) * 128], in0=av[:, :, 0:128],
                        in1=rcS[ai][:, 0:2].unsqueeze(2).to_broadcast([128, 2, 128]), op=ALU.mult),
                        [bPS[a], brc[ai]], [bX[2 * ai], bX[2 * ai + 1]])

        for h in range(4):
            items = []
            for mc in range(2):
                items.append((lambda h=h, mc=mc: mem_S(h, mc), lambda r, h=h, mc=mc: mem_PV(h, mc, r)))
            pipeline(items)
            gate_chain(h)
            if h < 3:
                conv_chunk(h + 1)

        deferred_casts()
        late_consts()
        nmb = [ps_rot(), ps_rot()]
        for st in range(NST):
            nmf = nmS[st][:].rearrange("p a b -> p (a b)")
            for p in range(4):
                b = nmb[p // 2]
                k.op("pe", lambda p=p, b=b: pe.transpose(out=psb16(b)[:, (p % 2) * 512 + st * 128:(p % 2) * 512 + (st + 1) * 128],
                                                         in_=nmf[:, p * 128:(p + 1) * 128], identity=ident[:]), [bnm[st], bC], [bPS[b]])
        for p in range(4):
            b = nmb[p // 2]
            for e in range(2):
                h = 2 * p + e
                evac_copy(QT(h)[64:80, :], psb16(b)[64 * e:64 * e + 16, (p % 2) * 512:(p % 2) * 512 + 512], [bPS[b]], [bQT[h]])

        nchunk = 4 * j + 4
        hstate = {}

        def moba_S(h, c):
            dg = c - 4 * j
            q0 = 0 if dg <= 0 else dg * 128
            b = ps_rot()
            k.op("pe", lambda: pe.matmul(PS[b][:, q0:512], lhsT=T_KT[0:80, h, c * 128:(c + 1) * 128], rhs=QT(h)[:, q0:512],
                                         start=True, stop=True), [bKT[h], bKT1[h], bQT[h]], [bPS[b]])
            if dg >= 0:
                k.op("pe", lambda: pe.matmul(PS[b][:, dg * 128:(dg + 1) * 128], lhsT=ident[:], rhs=tri[:], start=False, stop=True,
                                             skip_group_check=True), [bC, bC2], [bPS[b]])
            r = pti[0] % 3
            pti[0] += 1
            k.op("act", lambda: act.activation(out=PTv(r)[:, q0:512], in_=PS[b][:, q0:512], func=AF.Exp, scale=0.125), [bPS[b]], [bPT[r]])
            return (r, q0)

        def moba_PV(h, c, rq):
            r, q0 = rq
            if c == 0:
                a = ps_acc()
                hstate[h] = a
                k.op("pe", lambda: pe.matmul(PS[a][:, 0:260], lhsT=zer[:, 0:128], rhs=zer[:, 0:260], start=True, stop=True), [bC2], [bPS[a]])
            a = hstate[h]
            av = PS[a][:, 0:260].rearrange("p (a b) -> p a b", a=4)
            for st in range(q0 // 128, NST):
                k.op("pe", lambda st=st: pe.matmul(av[:, st, 0:65], lhsT=PTv(r)[:, st * 128:(st + 1) * 128], rhs=T_V[:, c, h, 0:65],
                                                   start=False, stop=True, skip_group_check=True), [bPT[r], bV], [bPS[a]])
            if c == nchunk - 1:
                ri = h % 2
                k.op("dve", lambda: dve.reciprocal(out=rcS[ri][:, 0:4], in_=av[:, :, 64]), [bPS[a]], [brc[ri]])
                k.op("dve", lambda: dve.tensor_tensor(out=T_X[:, :, h * 64:(h + 1) * 64], in0=av[:, :, 0:64],
                                                      in1=rcS[ri][:, 0:4].unsqueeze(2).to_broadcast([128, 4, 64]), op=ALU.mult),
                     [bPS[a], brc[ri]], bX)

        items = []
        for h in range(8):
            for c in range(nchunk):
                items.append((lambda h=h, c=c: moba_S(h, c), lambda rq, h=h, c=c: moba_PV(h, c, rq)))
        pipeline(items)
        for which in (0, 1):
            for cp in range(2):
                b = ps_rot()
                for c in (2 * cp, 2 * cp + 1):
                    for st in range(NST):
                        k.op("pe", lambda c=c, st=st, b=b: pe.transpose(
                            out=psb16(b)[:, (c % 2) * 512 + st * 128:(c % 2) * 512 + (st + 1) * 128],
                            in_=T_X[:, st, which * 512 + c * 128: which * 512 + (c + 1) * 128], identity=ident[:]), [bX[st], bC], [bPS[b]])
                for c in (2 * cp, 2 * cp + 1):
                    src = psb16(b)[:, (c % 2) * 512:(c % 2) * 512 + 512]
                    if which == 0:
                        evac_copy(yTm(c), src, [bPS[b]], [bY2[c // 2]])
                    else:
                        evac_copy(yTe(c), src, [bPS[b]], [bY2[2 + c // 2]])
        for st in range(NST):
            if seq == 0 and j == 0:
                k.dma("sp", f"xld{st}", lambda st=st: nc.sync.dma_start(out=xf(st), in_=x_d[seq, j * T + st * 128:j * T + (st + 1) * 128, :]), [], [bX[st]])
            else:
                k.dma("pool", f"xld{st}", lambda st=st: nc.gpsimd.dma_start(out=xf(st), in_=x_d[seq, j * T + st * 128:j * T + (st + 1) * 128, :]), [], [bX[st]])
        ysrc = ((yTc, [bYC]), (yTm, [bY2[0], bY2[1]]), (yTe, [bY2[2], bY2[3]]))
        for i in range(3):
            yfn, ybufs = ysrc[i]
            for dp in range(4):
                sg_, sb_ = w_in_grp(14 + 4 * i + dp), w_br_grp(i, dp)
                for c in (2 * dp, 2 * dp + 1):
                    off = (c % 2) * 128
                    bg, bp = ps_rot(), ps_rot()
                    for kc in range(8):
                        k.op("pe", lambda kc=kc: pe.matmul(PS[bg][:], lhsT=sg_[0][:, kc, off:off + 128], rhs=T_hT[:, kc, :], start=(kc == 0), stop=(kc == 7)),
                             bhT + [sg_[1]], [bPS[bg]])
                    for kc in range(4):
                        k.op("pe", lambda kc=kc: pe.matmul(PS[bp][:], lhsT=sb_[0][:, kc, off:off + 128], rhs=yfn(kc), start=(kc == 0), stop=(kc == 3)),
                             ybufs + [sb_[1]], [bPS[bp]])
                    ti = c % 2
                    k.op("act", lambda: act.activation(out=tmpf(ti), in_=PS[bg][:], func=AF.Sigmoid,
                                                       bias=colsA[:, CA["bgate"] + i * 8 + c:CA["bgate"] + i * 8 + c + 1], scale=1.0),
                         [bPS[bg], bC], [bTMP[ti]])
                    if i == 0:
                        k.op("dve", lambda: dve.tensor_tensor(out=macc(c), in0=PS[bp][:], in1=tmpf(ti), op=ALU.mult), [bPS[bp], bTMP[ti]], [bACC])
                    else:
                        k.op("dve", lambda: dve.tensor_tensor(out=tmpf(2 + ti), in0=PS[bp][:], in1=tmpf(ti), op=ALU.mult),
                             [bPS[bp], bTMP[ti]], [bTMP[2 + ti]])
                        k.op("pool", lambda: pool.tensor_tensor(out=macc(c), in0=macc(c), in1=tmpf(2 + ti), op=ALU.add), [bACC, bTMP[2 + ti]], [bACC])
        wos = [w_o_grp(g) for g in range(4)]
        for st in range(NST):
            for g in range(4):
                wv, wb = wos[g]
                b = ps_rot()
                for kc in range(8):
                    k.op("pe", lambda kc=kc: pe.matmul(PS[b][:, 0:256], lhsT=macc(kc)[:, st * 128:(st + 1) * 128], rhs=wv[:, kc, :],
                                                       start=(kc == 0), stop=(kc == 7)), [bACC, wb], [bPS[b]])
                k.op("dve", lambda: dve.tensor_tensor(out=xf(st)[:, g * 256:(g + 1) * 256], in0=PS[b][:, 0:256],
                                                      in1=xf(st)[:, g * 256:(g + 1) * 256], op=ALU.add), [bPS[b], bX[st]], [bX[st]])
            normA(st, [bX[st]])
            if st >= 1:
                normB(st - 1, CA["gffn"])
        normB(NST - 1, CA["gffn"])
        k.fence(ffn_bufs)
        fcw, fcb = 0, CA["fcb"]
        for gi in range(11):
            sa, sb_ = w_up_grp(gi), w_up_grp(11 + gi)
            for m in (2 * gi, 2 * gi + 1):
                off = (m % 2) * 128
                ba, bb = ps_rot(), ps_rot()
                for kc in range(8):
                    k.op("pe", lambda kc=kc: pe.matmul(PS[ba][:], lhsT=sa[0][:, kc, off:off + 128], rhs=T_hT[:, kc, :], start=(kc == 0), stop=(kc == 7)),
                         bhT + [sa[1]], [bPS[ba]])
                for kc in range(8):
                    k.op("pe", lambda kc=kc: pe.matmul(PS[bb][:], lhsT=sb_[0][:, kc, off:off + 128], rhs=T_hT[:, kc, :], start=(kc == 0), stop=(kc == 7)),
                         bhT + [sb_[1]], [bPS[bb]])
                a_ = asb(m % 2)
                ab_ = bASB[m % 2]
                k.op("act", lambda: act.copy(out=a_[:, 2:514], in_=PS[ba][:]), [bPS[ba]], [ab_])
                k.op("pool", lambda: pool.tensor_copy(out=a_[:, 0:2], in_=ah[:, m, :]), [bah], [ab_])
                k.op("pool", lambda: pool.tensor_copy(out=ah[:, m, :], in_=a_[:, 512:514]), [ab_], [bah])
                k.op("act", lambda: act.activation(out=tmpf(0), in_=PS[ba][:], func=AF.Identity, bias=colsA[:, fcb + m:fcb + m + 1],
                                                   scale=colsB[:, fcw + 44 + m:fcw + 44 + m + 1]), [bPS[ba], bC], [bTMP[0]])
                k.op("dve", lambda: dve.scalar_tensor_tensor(out=tmpf(1), in0=a_[:, 1:513], scalar=colsB[:, fcw + 22 + m:fcw + 22 + m + 1], in1=tmpf(0),
                                                             op0=ALU.mult, op1=ALU.add), [ab_, bC, bTMP[0]], [bTMP[1]])
                k.op("dve", lambda: dve.scalar_tensor_tensor(out=tmpf(2), in0=a_[:, 0:512], scalar=colsB[:, fcw + m:fcw + m + 1], in1=tmpf(1),
                                                             op0=ALU.mult, op1=ALU.add), [ab_, bC, bTMP[1]], [bTMP[2]])
                k.op("act", lambda: act.activation(out=tmpf(3), in_=tmpf(2), func=AF.Silu), [bTMP[2]], [bTMP[3]])
                k.op("dve", lambda: dve.tensor_tensor(out=actT(m), in0=PS[bb][:], in1=tmpf(3), op=ALU.mult), [bPS[bb], bTMP[3]], [bACT])
        for cg in range(4):
            banks = [ps_rot() for _ in range(NST)]
            for (m0, cnt) in ((0, 8), (8, 8), (16, 6)):
                wv, wb = w_dn_grp(cg, m0, cnt)
                for mi in range(cnt):
                    m = m0 + mi
                    for st in range(NST):
                        b = banks[st]
                        k.op("pe", lambda m=m, mi=mi, st=st, b=b: pe.matmul(PS[b][:, 0:256], lhsT=actT(m)[:, st * 128:(st + 1) * 128], rhs=wv[:, mi, :],
                                                                            start=(m == 0), stop=(m == NCH_FF - 1)), [bACT, wb], [bPS[b]])
            for st in range(NST):
                b = banks[st]
                k.op("dve", lambda b=b, st=st: dve.tensor_tensor(out=xf(st)[:, cg * 256:(cg + 1) * 256], in0=PS[b][:, 0:256],
                                                                 in1=xf(st)[:, cg * 256:(cg + 1) * 256], op=ALU.add), [bPS[b], bX[st]], [bX[st]])
            if do_early:
                if cg == 0:
                    early_A(seq, j + 1, 0)
                    early_A(seq, j + 1, 1)
                elif cg == 1:
                    early_B(0)
                    early_B(1)
                    early_A(seq, j + 1, 2)
                    early_A(seq, j + 1, 3)
                elif cg == 2:
                    early_B(2)
                    early_B(3)
        for st in range(NST):
            k.dma("pool", f"ost{st}", lambda st=st: nc.gpsimd.dma_start(out=out_d[seq, j * T + st * 128:j * T + (st + 1) * 128, :], in_=xf(st)), [bX[st]], [])

    for seq in range(NSEQ):
        if seq > 0:
            k.fence(mixer_bufs)
        mem_prologue(seq)
        for j in range(NT):
            tile(seq, j, p0_done=(j > 0), do_early=(j < NT - 1))
    k.final_wait("sp")
    k.final_wait("pool")
    es.close()
    return nc


_NAMES = ["x", "mem", "g_mix", "w_in", "b_gate", "conv_w", "conv_b", "moba_q_gain", "moba_k_gain", "g_mem", "w_mem_kv",
          "memq_gain", "memk_gain", "w_br_conv", "w_br_moba", "w_br_mem", "w_o", "g_ffn", "w_up", "ffn_conv_w",
          "ffn_conv_b", "w_down"]


def kernel(**inputs):
    ncores = 8
    x = np.ascontiguousarray(np.asarray(inputs["x"], dtype=np.float32))
    mem = np.ascontiguousarray(np.asarray(inputs["mem"], dtype=np.float32))
    bsz, s, _ = x.shape
    per = bsz // ncores
    nc = build_nc(per, s // T)
    shared = {n: np.ascontiguousarray(np.asarray(inputs[n], dtype=np.float32)) for n in _NAMES[2:]}
    in_maps = []
    for c in range(ncores):
        m = dict(shared)
        m["x"] = x[c * per:(c + 1) * per]
        m["mem"] = mem[c * per:(c + 1) * per]
        in_maps.append(m)
    res = run_bass_kernel_spmd(nc, in_maps, core_ids=list(range(ncores)))
    return np.concatenate([np.asarray(r["out"]) for r in res.results], axis=0).astype(np.float32)
```
